# Optimizing a Trainium2 kernel written in Bass

```python
import math
import jax, jax.numpy as jnp
from jax import lax
import numpy as np

D_MODEL = 1024
BATCH = 4
SEQ = 4096
DEPTH = 2

MEM_LEN = 256
D_FF = 11 * D_MODEL // 4
NORM_EPS = 1e-6
FFN_HALF = 0.5
CHUNK = 64
N_BRANCH = 4
BRANCH_WIDTH = D_MODEL // 2

MLSTM_HEADS = 4
MLSTM_HEAD_DIM = BRANCH_WIDTH // MLSTM_HEADS
MLSTM_CONV = 4

S5_GROUP = 16
S5_GROUPS = BRANCH_WIDTH // S5_GROUP
S5_STATE = 64
S5_MIN_NEG = 1e-4

GLA_HEADS = 4
GLA_KEY_WIDTH = BRANCH_WIDTH // 2
GLA_HEAD_K = GLA_KEY_WIDTH // GLA_HEADS
GLA_HEAD_V = BRANCH_WIDTH // GLA_HEADS
GLA_GATE_RANK = 16
GLA_GATE_TAU = 16.0

RWKV_HEAD = 64
RWKV_HEADS = BRANCH_WIDTH // RWKV_HEAD
RWKV_DECAY_RANK = 64
RWKV_ICLR_RANK = 64
RWKV_GATE_RANK = 128
RWKV_GN_EPS = 64e-5
RWKV_WIDTHS = (BRANCH_WIDTH, BRANCH_WIDTH, BRANCH_WIDTH, RWKV_DECAY_RANK, RWKV_ICLR_RANK, RWKV_GATE_RANK)
RWKV_COLS = sum(RWKV_WIDTHS)
RWKV_SPLIT_POINTS = tuple(int(s) for s in np.cumsum(RWKV_WIDTHS)[:-1])

XATTN_HEADS = 4
XATTN_HEAD_DIM = D_MODEL // XATTN_HEADS

IN_WIDTHS = (
    BRANCH_WIDTH, BRANCH_WIDTH, MLSTM_HEADS, MLSTM_HEADS,
    BRANCH_WIDTH,
    GLA_KEY_WIDTH, GLA_KEY_WIDTH, BRANCH_WIDTH, BRANCH_WIDTH,
    GLA_GATE_RANK,
    RWKV_COLS,
    N_BRANCH * D_MODEL,
)
N_IN = sum(IN_WIDTHS)
IN_SPLIT_POINTS = tuple(int(s) for s in np.cumsum(IN_WIDTHS)[:-1])

kernel_name = "hybrid_parallel_gated_mixer_trunk"


def rms_norm(x, gain):
    x32 = x.astype(jnp.float32)
    y = x32 * lax.rsqrt(jnp.mean(x32 * x32, axis=-1, keepdims=True) + NORM_EPS)
    return (y * gain.astype(jnp.float32)).astype(x.dtype)


def head_rms_norm(x, gain, n_heads):
    shp = x.shape
    xh = x.astype(jnp.float32).reshape(shp[:-1] + (n_heads, -1))
    y = xh * lax.rsqrt(jnp.mean(xh * xh, axis=-1, keepdims=True) + NORM_EPS)
    return y.reshape(shp) * gain.astype(jnp.float32)


def head_group_norm(x, gain, n_heads, eps):
    shp = x.shape
    xh = x.astype(jnp.float32).reshape(shp[:-1] + (n_heads, -1))
    mu = jnp.mean(xh, axis=-1, keepdims=True)
    xc = xh - mu
    y = xc * lax.rsqrt(jnp.mean(xc * xc, axis=-1, keepdims=True) + eps)
    return y.reshape(shp) * gain.astype(jnp.float32)


def swiglu_ffn(x, w_gate, w_up, w_down):
    return (jax.nn.silu(x @ w_gate) * (x @ w_up)) @ w_down


def causal_depthwise_conv(x, w):
    k_len = w.shape[0]
    t = x.shape[1]
    xp = jnp.pad(x, ((0, 0), (k_len - 1, 0), (0, 0)))
    return sum(xp[:, j:j + t] * w[j] for j in range(k_len))


def token_shift(x):
    return jnp.pad(x, ((0, 0), (1, 0), (0, 0)))[:, :-1, :]


def to_chunks4(x):
    b, t, h, e = x.shape
    return x.reshape(b, t // CHUNK, CHUNK, h, e).transpose(1, 0, 3, 2, 4)


def to_chunks3(x):
    b, t, h = x.shape
    return x.reshape(b, t // CHUNK, CHUNK, h).transpose(1, 0, 3, 2)


def from_chunks4(y):
    nc, b, h, c, e = y.shape
    return y.transpose(1, 0, 3, 2, 4).reshape(b, nc * c, h, e)


def mlstm_chunkwise(q, k, v, log_i, log_f):
    bsz, _, n_h, e = q.shape
    mask = jnp.tril(jnp.ones((CHUNK, CHUNK), dtype=bool))

    def step(carry, xs):
        c_st, n_st, m_st = carry
        qc, kc, vc, ic, fc = xs
        b = jnp.cumsum(fc, axis=-1)
        d = jnp.where(mask, b[..., :, None] - b[..., None, :] + ic[..., None, :], -jnp.inf)
        m_inter = b + m_st[..., None]
        m_row = jnp.maximum(m_inter, jnp.max(d, axis=-1))
        w = jnp.exp(d - m_row[..., None]) * jnp.einsum('bhte,bhse->bhts', qc, kc)
        inter = jnp.exp(m_inter - m_row)
        num = inter[..., None] * jnp.einsum('bhte,bhev->bhtv', qc, c_st) + jnp.einsum('bhts,bhsv->bhtv', w, vc)
        den = inter * jnp.einsum('bhte,bhe->bht', qc, n_st) + jnp.sum(w, axis=-1)
        h = num / jnp.maximum(jnp.abs(den), jnp.exp(-m_row))[..., None]
        b_last = b[..., -1]
        m_new = jnp.maximum(b_last + m_st, jnp.max(b_last[..., None] - b + ic, axis=-1))
        decay = jnp.exp(b_last + m_st - m_new)
        wk = jnp.exp(b_last[..., None] - b + ic - m_new[..., None])
        c_new = decay[..., None, None] * c_st + jnp.einsum('bhs,bhse,bhsv->bhev', wk, kc, vc)
        n_new = decay[..., None] * n_st + jnp.einsum('bhs,bhse->bhe', wk, kc)
        return (c_new, n_new, m_new), h

    init = (jnp.zeros((bsz, n_h, e, e), jnp.float32),
            jnp.zeros((bsz, n_h, e), jnp.float32),
            jnp.zeros((bsz, n_h), jnp.float32))
    _, h = lax.scan(step, init, (to_chunks4(q), to_chunks4(k), to_chunks4(v),
                                 to_chunks3(log_i), to_chunks3(log_f)))
    return from_chunks4(h)


def gla_chunkwise(q, k, v, log_a):
    bsz, _, n_h, dk = q.shape
    dv = v.shape[-1]
    mask = jnp.tril(jnp.ones((CHUNK, CHUNK), dtype=bool))[..., None]

    def step(s_st, xs):
        qc, kc, vc, ac = xs
        b = jnp.cumsum(ac, axis=2)
        inter = jnp.einsum('bhtk,bhkv->bhtv', qc * jnp.exp(b), s_st)
        diff = jnp.where(mask, b[:, :, :, None, :] - b[:, :, None, :, :], -jnp.inf)
        attn = jnp.einsum('bhtk,bhsk,bhtsk->bhts', qc, kc, jnp.exp(diff))
        o = inter + jnp.einsum('bhts,bhsv->bhtv', attn, vc)
        b_last = b[:, :, -1:, :]
        s_new = (jnp.exp(b_last[:, :, 0, :])[..., None] * s_st
                 + jnp.einsum('bhsk,bhsv->bhkv', kc * jnp.exp(b_last - b), vc))
        return s_new, o

    init = jnp.zeros((bsz, n_h, dk, dv), jnp.float32)
    _, o = lax.scan(step, init, (to_chunks4(q), to_chunks4(k), to_chunks4(v), to_chunks4(log_a)))
    return from_chunks4(o)


def rwkv7_scan(r, w, k, v, a, b):
    bsz, _, n_h, n = r.shape

    def step(s, xs):
        rt, wt, kt, vt, at, bt = xs
        sa = jnp.einsum('bhij,bhj->bhi', s, at)
        s = s * wt[:, :, None, :] + sa[..., None] * bt[:, :, None, :] + vt[..., None] * kt[:, :, None, :]
        return s, jnp.einsum('bhij,bhj->bhi', s, rt)

    xs = tuple(jnp.moveaxis(z, 1, 0) for z in (r, w, k, v, a, b))
    _, y = lax.scan(step, jnp.zeros((bsz, n_h, n, n), jnp.float32), xs)
    return jnp.moveaxis(y, 0, 1)


def complex_affine_combine(e1, e2):
    a1r, a1i, b1r, b1i = e1
    a2r, a2i, b2r, b2i = e2
    return (a2r * a1r - a2i * a1i,
            a2r * a1i + a2i * a1r,
            a2r * b1r - a2i * b1i + b2r,
            a2r * b1i + a2i * b1r + b2i)


def mlstm_branch(u, o_pre, i_pre, f_pre, conv_w, wq, wk, wv, b_i, b_f, norm_g, proj):
    bsz, t, _ = u.shape
    f32 = jnp.float32
    uc = jax.nn.silu(causal_depthwise_conv(u, conv_w))
    uc_h = uc.reshape(bsz, t, MLSTM_HEADS, MLSTM_HEAD_DIM)
    u_h = u.reshape(bsz, t, MLSTM_HEADS, MLSTM_HEAD_DIM)
    q = jnp.einsum('bthe,hef->bthf', uc_h, wq).astype(f32)
    k = jnp.einsum('bthe,hef->bthf', uc_h, wk).astype(f32) * MLSTM_HEAD_DIM ** -0.5
    v = jnp.einsum('bthe,hef->bthf', u_h, wv).astype(f32)
    log_i = (i_pre + b_i).astype(f32)
    log_f = jax.nn.log_sigmoid((f_pre + b_f).astype(f32))
    h = mlstm_chunkwise(q, k, v, log_i, log_f).reshape(bsz, t, BRANCH_WIDTH)
    h = head_rms_norm(h, norm_g, MLSTM_HEADS) * jax.nn.sigmoid(o_pre.astype(f32))
    return h.astype(u.dtype) @ proj


def s5_branch(u, a_re, a_im, log_step, b_re, b_im, c_re, c_im, d_skip, glu_w1, glu_w2):
    bsz, t, _ = u.shape
    f32 = jnp.float32
    u32 = u.astype(f32)
    xg = u32.reshape(bsz, t, S5_GROUPS, S5_GROUP)
    step = jnp.exp(log_step.astype(f32))[:, None]
    lam_re = jnp.minimum(a_re.astype(f32), -S5_MIN_NEG)
    lam_im = a_im.astype(f32)
    mag = jnp.exp(lam_re * step)
    bar_re = mag * jnp.cos(lam_im * step)
    bar_im = mag * jnp.sin(lam_im * step)
    denom = lam_re * lam_re + lam_im * lam_im
    coef_re = ((bar_re - 1.0) * lam_re + bar_im * lam_im) / denom
    coef_im = (bar_im * lam_re - (bar_re - 1.0) * lam_im) / denom
    br, bi = b_re.astype(f32), b_im.astype(f32)
    bb_re = coef_re[..., None] * br - coef_im[..., None] * bi
    bb_im = coef_re[..., None] * bi + coef_im[..., None] * br
    bu_re = jnp.einsum('btgc,gpc->btgp', xg, bb_re)
    bu_im = jnp.einsum('btgc,gpc->btgp', xg, bb_im)
    lr = jnp.broadcast_to(bar_re, bu_re.shape)
    li = jnp.broadcast_to(bar_im, bu_im.shape)
    _, _, s_re, s_im = lax.associative_scan(complex_affine_combine, (lr, li, bu_re, bu_im), axis=1)
    y = (jnp.einsum('btgp,gcp->btgc', s_re, c_re.astype(f32))
         - jnp.einsum('btgp,gcp->btgc', s_im, c_im.astype(f32)))
    y = y.reshape(bsz, t, BRANCH_WIDTH) + d_skip.astype(f32) * u32
    z = jax.nn.gelu(y).astype(u.dtype)
    return (z @ glu_w1) * jax.nn.sigmoid(z @ glu_w2)


def gla_branch(q, k, v, g, a_low, a_up, a_bias, norm_g, proj):
    bsz, t, _ = q.shape
    f32 = jnp.float32
    qh = q.astype(f32).reshape(bsz, t, GLA_HEADS, GLA_HEAD_K) * GLA_HEAD_K ** -0.5
    kh = k.astype(f32).reshape(bsz, t, GLA_HEADS, GLA_HEAD_K)
    vh = v.astype(f32).reshape(bsz, t, GLA_HEADS, GLA_HEAD_V)
    log_a = jax.nn.log_sigmoid((a_low @ a_up + a_bias).astype(f32)) / GLA_GATE_TAU
    log_a = log_a.reshape(bsz, t, GLA_HEADS, GLA_HEAD_K)
    o = gla_chunkwise(qh, kh, vh, log_a).reshape(bsz, t, BRANCH_WIDTH)
    o = head_rms_norm(o, norm_g, GLA_HEADS) * jax.nn.silu(g.astype(f32))
    return o.astype(q.dtype) @ proj


def rwkv7_branch(z, mu, w0, w_up, a0, a_up, g_up, k_k, k_a, r_k, norm_g, proj):
    bsz, t, _ = z.shape
    f32 = jnp.float32
    z = z + mu * (token_shift(z) - z)
    r, k, v, xw, xa, xg = jnp.split(z, RWKV_SPLIT_POINTS, axis=-1)
    w_log = -jax.nn.softplus(-(w0 + jnp.tanh(xw) @ w_up).astype(f32)) - 0.5
    decay = jnp.exp(-jnp.exp(w_log))
    a = jax.nn.sigmoid((a0 + xa @ a_up).astype(f32))
    g = jax.nn.sigmoid(xg) @ g_up

    def heads(y):
        return y.astype(f32).reshape(bsz, t, RWKV_HEADS, RWKV_HEAD)

    kk = heads(k * k_k)
    kk = kk / jnp.maximum(jnp.linalg.norm(kk, axis=-1, keepdims=True), 1e-12)
    k_rep = heads(k.astype(f32) * (1.0 + (a - 1.0) * k_a.astype(f32)))
    rh, vh, ah = heads(r), heads(v), heads(a)
    y = rwkv7_scan(rh, heads(decay), k_rep, vh, -kk, kk * ah)
    y = head_group_norm(y.reshape(bsz, t, BRANCH_WIDTH), norm_g, RWKV_HEADS, RWKV_GN_EPS)
    bonus = jnp.sum(rh * k_rep * r_k.astype(f32), axis=-1, keepdims=True) * vh
    y = (y + bonus.reshape(bsz, t, BRANCH_WIDTH)) * g.astype(f32)
    return y.astype(z.dtype) @ proj


def memory_cross_attention(h, mem_n, wq, wk, wv, wo):
    bsz, t, _ = h.shape
    m_len = mem_n.shape[1]
    q = (h @ wq).reshape(bsz, t, XATTN_HEADS, XATTN_HEAD_DIM)
    k = (mem_n @ wk).reshape(bsz, m_len, XATTN_HEADS, XATTN_HEAD_DIM)
    v = (mem_n @ wv).reshape(bsz, m_len, XATTN_HEADS, XATTN_HEAD_DIM)
    s = jnp.einsum('bthd,bmhd->bhtm', q, k).astype(jnp.float32) * XATTN_HEAD_DIM ** -0.5
    p = jax.nn.softmax(s, axis=-1).astype(h.dtype)
    o = jnp.einsum('bhtm,bmhd->bthd', p, v).reshape(bsz, t, D_MODEL)
    return o @ wo


def setup_inputs(seed: int = 0) -> dict:
    key = jax.random.key(seed)
    ks = iter(jax.random.split(key, 80))
    f32 = jnp.float32
    L, D, W, F = DEPTH, D_MODEL, BRANCH_WIDTH, D_FF

    def nrm(shape, scale):
        return jax.random.normal(next(ks), shape, f32) * scale

    def gain(shape):
        return 1.0 + nrm(shape, 0.02)

    def unif(shape, lo, hi):
        return jax.random.uniform(next(ks), shape, f32, lo, hi)

    p = {}
    p["x"] = nrm((BATCH, SEQ, D), 1.0)
    p["mem"] = nrm((BATCH, MEM_LEN, D), 1.0)
    p["ffn1_norm"] = gain((L, D))
    p["ffn1_w_gate"] = nrm((L, D, F), D ** -0.5)
    p["ffn1_w_up"] = nrm((L, D, F), D ** -0.5)
    p["ffn1_w_down"] = nrm((L, F, D), F ** -0.5)
    p["mix_norm"] = gain((L, D))
    p["w_in"] = nrm((L, D, N_IN), D ** -0.5)
    p["gate_bias"] = nrm((L, N_BRANCH * D), 0.1)
    p["mlstm_conv"] = nrm((L, MLSTM_CONV, W), MLSTM_CONV ** -0.5)
    p["mlstm_wq"] = nrm((L, MLSTM_HEADS, MLSTM_HEAD_DIM, MLSTM_HEAD_DIM), MLSTM_HEAD_DIM ** -0.5)
    p["mlstm_wk"] = nrm((L, MLSTM_HEADS, MLSTM_HEAD_DIM, MLSTM_HEAD_DIM), MLSTM_HEAD_DIM ** -0.5)
    p["mlstm_wv"] = nrm((L, MLSTM_HEADS, MLSTM_HEAD_DIM, MLSTM_HEAD_DIM), MLSTM_HEAD_DIM ** -0.5)
    p["mlstm_b_i"] = nrm((L, MLSTM_HEADS), 0.1)
    p["mlstm_b_f"] = jnp.linspace(3.0, 6.0, MLSTM_HEADS, dtype=f32)[None, :] + nrm((L, MLSTM_HEADS), 0.1)
    p["mlstm_norm"] = gain((L, W))
    p["mlstm_proj"] = nrm((L, W, D), W ** -0.5)
    p["s5_a_re"] = -0.5 + nrm((L, S5_GROUPS, S5_STATE), 0.01)
    p["s5_a_im"] = (jnp.broadcast_to(math.pi * jnp.arange(S5_STATE, dtype=f32), (L, S5_GROUPS, S5_STATE))
                    + nrm((L, S5_GROUPS, S5_STATE), 0.01))
    p["s5_log_step"] = unif((L, S5_GROUPS), math.log(1e-3), math.log(1e-1))
    p["s5_b_re"] = nrm((L, S5_GROUPS, S5_STATE, S5_GROUP), (2 * S5_GROUP) ** -0.5)
    p["s5_b_im"] = nrm((L, S5_GROUPS, S5_STATE, S5_GROUP), (2 * S5_GROUP) ** -0.5)
    p["s5_c_re"] = nrm((L, S5_GROUPS, S5_GROUP, S5_STATE), S5_STATE ** -0.5)
    p["s5_c_im"] = nrm((L, S5_GROUPS, S5_GROUP, S5_STATE), S5_STATE ** -0.5)
    p["s5_d"] = nrm((L, W), 1.0)
    p["s5_glu_w1"] = nrm((L, W, D), W ** -0.5)
    p["s5_glu_w2"] = nrm((L, W, D), W ** -0.5)
    p["gla_a_up"] = nrm((L, GLA_GATE_RANK, GLA_KEY_WIDTH), GLA_GATE_RANK ** -0.5)
    p["gla_a_bias"] = nrm((L, GLA_KEY_WIDTH), 0.1)
    p["gla_norm"] = gain((L, W))
    p["gla_proj"] = nrm((L, W, D), W ** -0.5)
    p["rwkv_mu"] = unif((L, RWKV_COLS), 0.0, 1.0)
    p["rwkv_w0"] = jnp.linspace(-6.5, -1.5, W, dtype=f32)[None, :] + nrm((L, W), 0.1)
    p["rwkv_w_up"] = nrm((L, RWKV_DECAY_RANK, W), 0.1)
    p["rwkv_a0"] = nrm((L, W), 0.1)
    p["rwkv_a_up"] = nrm((L, RWKV_ICLR_RANK, W), 0.1)
    p["rwkv_g_up"] = nrm((L, RWKV_GATE_RANK, W), RWKV_GATE_RANK ** -0.5)
    p["rwkv_k_k"] = 0.85 + nrm((L, W), 0.02)
    p["rwkv_k_a"] = 1.0 + nrm((L, W), 0.02)
    p["rwkv_r_k"] = nrm((L, RWKV_HEADS, RWKV_HEAD), 0.1)
    p["rwkv_norm"] = gain((L, W))
    p["rwkv_proj"] = nrm((L, W, D), W ** -0.5)
    p["w_out"] = nrm((L, D, D), D ** -0.5)
    p["xattn_norm"] = gain((L, D))
    p["mem_norm"] = gain((L, D))
    p["xattn_wq"] = nrm((L, D, D), D ** -0.5)
    p["xattn_wk"] = nrm((L, D, D), D ** -0.5)
    p["xattn_wv"] = nrm((L, D, D), D ** -0.5)
    p["xattn_wo"] = nrm((L, D, D), D ** -0.5)
    p["ffn2_norm"] = gain((L, D))
    p["ffn2_w_gate"] = nrm((L, D, F), D ** -0.5)
    p["ffn2_w_up"] = nrm((L, D, F), D ** -0.5)
    p["ffn2_w_down"] = nrm((L, F, D), F ** -0.5)
    p["final_norm"] = gain((D,))
    return p


def reference(x, mem,
              ffn1_norm, ffn1_w_gate, ffn1_w_up, ffn1_w_down,
              mix_norm, w_in, gate_bias,
              mlstm_conv, mlstm_wq, mlstm_wk, mlstm_wv, mlstm_b_i, mlstm_b_f, mlstm_norm, mlstm_proj,
              s5_a_re, s5_a_im, s5_log_step, s5_b_re, s5_b_im, s5_c_re, s5_c_im, s5_d, s5_glu_w1, s5_glu_w2,
              gla_a_up, gla_a_bias, gla_norm, gla_proj,
              rwkv_mu, rwkv_w0, rwkv_w_up, rwkv_a0, rwkv_a_up, rwkv_g_up, rwkv_k_k, rwkv_k_a, rwkv_r_k,
              rwkv_norm, rwkv_proj,
              w_out,
              xattn_norm, mem_norm, xattn_wq, xattn_wk, xattn_wv, xattn_wo,
              ffn2_norm, ffn2_w_gate, ffn2_w_up, ffn2_w_down,
              final_norm):
    bsz, t, _ = x.shape
    dt = x.dtype
    for l in range(DEPTH):
        h = rms_norm(x, ffn1_norm[l])
        x = x + FFN_HALF * swiglu_ffn(h, ffn1_w_gate[l], ffn1_w_up[l], ffn1_w_down[l])

        h = rms_norm(x, mix_norm[l])
        z = h @ w_in[l]
        (a_u, a_o, a_i, a_f, b_u, c_q, c_k, c_v, c_g, c_a, d_z, gate_pre) = jnp.split(
            z, IN_SPLIT_POINTS, axis=-1)
        y_a = mlstm_branch(a_u, a_o, a_i, a_f, mlstm_conv[l], mlstm_wq[l], mlstm_wk[l], mlstm_wv[l],
                           mlstm_b_i[l], mlstm_b_f[l], mlstm_norm[l], mlstm_proj[l])
        y_b = s5_branch(b_u, s5_a_re[l], s5_a_im[l], s5_log_step[l], s5_b_re[l], s5_b_im[l],
                        s5_c_re[l], s5_c_im[l], s5_d[l], s5_glu_w1[l], s5_glu_w2[l])
        y_c = gla_branch(c_q, c_k, c_v, c_g, c_a, gla_a_up[l], gla_a_bias[l], gla_norm[l], gla_proj[l])
        y_d = rwkv7_branch(d_z, rwkv_mu[l], rwkv_w0[l], rwkv_w_up[l], rwkv_a0[l], rwkv_a_up[l],
                           rwkv_g_up[l], rwkv_k_k[l], rwkv_k_a[l], rwkv_r_k[l], rwkv_norm[l], rwkv_proj[l])
        gates = jax.nn.sigmoid((gate_pre + gate_bias[l]).astype(jnp.float32)).astype(dt)
        gates = gates.reshape(bsz, t, N_BRANCH, D_MODEL)
        merged = (gates[:, :, 0] * y_a + gates[:, :, 1] * y_b
                  + gates[:, :, 2] * y_c + gates[:, :, 3] * y_d)
        x = x + merged @ w_out[l]

        h = rms_norm(x, xattn_norm[l])
        m = rms_norm(mem, mem_norm[l])
        x = x + memory_cross_attention(h, m, xattn_wq[l], xattn_wk[l], xattn_wv[l], xattn_wo[l])

        h = rms_norm(x, ffn2_norm[l])
        x = x + FFN_HALF * swiglu_ffn(h, ffn2_w_gate[l], ffn2_w_up[l], ffn2_w_down[l])
    return rms_norm(x, final_norm)
```

```python
import numpy as np
import concourse.bass as bass
import concourse.mybir as mybir
from concourse.bass_utils import run_bass_kernel_spmd

F32 = mybir.dt.float32
BF16 = mybir.dt.bfloat16
I32 = mybir.dt.int32
AF = mybir.ActivationFunctionType
ALU = mybir.AluOpType
AX = mybir.AxisListType


class Buf:
    def __init__(self, t=None):
        self.t = t
        self.w = None
        self.r = {}
        self.excl = False

    def sub(self):
        b = Buf(self.t)
        return b


class Sched:
    def __init__(self, nc, n_dma=32):
        self.nc = nc
        self.eng = {"pe": nc.tensor, "dve": nc.vector, "act": nc.scalar,
                    "pool": nc.gpsimd, "sp": nc.sync}
        self.sem = {e: nc.alloc_semaphore(name=f"s_{e}") for e in self.eng}
        self.cnt = {e: 0 for e in self.eng}
        self.waited = {e: {} for e in self.eng}
        self.qrange = {"sp": (0, n_dma // 2), "pool": (n_dma // 2, n_dma)}
        self.dsem = [nc.alloc_semaphore(name=f"dq{i}") for i in range(n_dma)]
        self.dval = [0] * n_dma
        self.dnext = {"sp": 0, "pool": n_dma // 2}
        self.nid = 0
        self.n_ins = 0
        self.stack = None

    def sb(self, name, shape, dt):
        self.nid += 1
        if self.stack is not None:
            return Buf(self.stack.enter_context(self.nc.sbuf_tensor(f"{name}_{self.nid}", list(shape), dt)))
        return Buf(self.nc.alloc_sbuf_tensor(f"{name}_{self.nid}", list(shape), dt))

    def ps(self, name, shape, dt):
        self.nid += 1
        b = Buf(self.nc.alloc_psum_tensor(f"{name}_{self.nid}", list(shape), dt))
        b.excl = True
        return b

    def _deps(self, reads, writes, e=None):
        need = {}

        def add(ev):
            if ev is None:
                return
            k, s, v = ev
            if k not in need or need[k][1] < v:
                need[k] = (s, v)

        for b in reads:
            add(b.w)
            if b.excl:
                for k, (s, v) in b.r.items():
                    if k != e:
                        add((k, s, v))
        for b in writes:
            add(b.w)
            for k, (s, v) in b.r.items():
                add((k, s, v))
        return need

    def _wait(self, e, need):
        eng = self.eng[e]
        wd = self.waited[e]
        for k, (s, v) in need.items():
            if k == "pe" and e == "pe":
                continue
            if wd.get(k, 0) < v:
                eng.wait_ge(s, v)
                wd[k] = v

    def _mark(self, ev, reads, writes):
        k, s, v = ev
        for b in writes:
            b.w = ev
            b.r = {}
        for b in reads:
            b.r[k] = (s, v)

    def op(self, e, fn, reads=(), writes=()):
        self._wait(e, self._deps(reads, writes, e))
        ins = fn(self.eng[e])
        self.cnt[e] += 1
        ins.then_inc(self.sem[e], 1)
        self.n_ins += 1
        self._mark((e, self.sem[e], self.cnt[e]), reads, writes)
        return ins

    def dma(self, out, in_, reads=(), writes=(), q="sp", **kw):
        i = self.dnext[q]
        lo, hi = self.qrange[q]
        self.dnext[q] = lo + (i + 1 - lo) % (hi - lo)
        need = self._deps(reads, writes, q)
        key = ("dma", i)
        if self.dval[i] > 0:
            need[key] = (self.dsem[i], self.dval[i])
        self._wait(q, need)
        ins = self.eng[q].dma_start(out=out, in_=in_, **kw)
        self.dval[i] += 16
        ins.then_inc(self.dsem[i], 16)
        self.n_ins += 1
        self._mark((key, self.dsem[i], self.dval[i]), reads, writes)
        return ins

    def coll(self, kind, ins, outs, groups, reads=(), writes=()):
        q = "pool"
        i = self.dnext[q]
        lo, hi = self.qrange[q]
        self.dnext[q] = lo + (i + 1 - lo) % (hi - lo)
        need = self._deps(reads, writes, q)
        key = ("dma", i)
        if self.dval[i] > 0:
            need[key] = (self.dsem[i], self.dval[i])
        self._wait(q, need)
        ins_ = self.nc.gpsimd.collective_compute(kind, ALU.bypass, replica_groups=groups, ins=ins, outs=outs)
        self.dval[i] += 16
        ins_.then_inc(self.dsem[i], 16)
        self.n_ins += 1
        self._mark((key, self.dsem[i], self.dval[i]), reads, writes)
        return ins_

    def split(self, buf, views):
        out = []
        for v in views:
            b = Buf(v)
            b.w = buf.w
            b.r = dict(buf.r)
            b.excl = buf.excl
            out.append(b)
        return out

    def join(self, buf, subs):
        for b in subs:
            evs = list(b.r.items())
            if b.w is not None:
                evs.append((b.w[0], (b.w[1], b.w[2])))
            for k, (sm, v) in evs:
                if k not in buf.r or buf.r[k][1] < v:
                    buf.r[k] = (sm, v)

    def barrier(self):
        for e in self.eng:
            need = {}
            for e2 in self.eng:
                if e2 != e and self.cnt[e2] > 0:
                    need[e2] = (self.sem[e2], self.cnt[e2])
            for i, v in enumerate(self.dval):
                if v > 0:
                    need[("dma", i)] = (self.dsem[i], v)
            wd = self.waited[e]
            for k, (s, v) in need.items():
                if wd.get(k, 0) < v:
                    self.eng[e].wait_ge(s, v)
                    wd[k] = v

    def finish(self):
        self.barrier()

    def o(self, e, name, reads=(), writes=(), **kw):
        self._wait(e, self._deps(reads, writes, e))
        ins = getattr(self.eng[e], name)(**kw)
        self.cnt[e] += 1
        ins.then_inc(self.sem[e], 1)
        self.n_ins += 1
        self._mark((e, self.sem[e], self.cnt[e]), reads, writes)
        return ins

    def mm(self, out, lhsT, rhs, start=True, stop=True, reads=(), writes=()):
        return self.o("pe", "matmul", reads, writes, out=out, lhsT=lhsT, rhs=rhs,
                      start=start, stop=stop)


D_MODEL = 1024
D_FF = 2816
NFC = D_FF // 128
EPS = 1e-6
W_BR = 512


def vec_fm(v):
    v = np.asarray(v, np.float32)
    return np.ascontiguousarray(v.reshape(-1, 128).T)


class Prog:
    GLOBAL = ("ident", "uincl", "ustrict", "lstrict", "rm64", "rm128", "bones")

    def __init__(self, num_devices=None):
        if num_devices:
            self.nc = bass.Bass("TRN2", target_bir_lowering=False, num_devices=num_devices)
        else:
            self.nc = bass.Bass("TRN2", target_bir_lowering=False)
        self.S = Sched(self.nc)
        self.dram = {}
        self.pre = ""
        S = self.S
        self.P = [S.ps(f"P{i}", [128, 512], F32) for i in range(7)]
        self.PB = S.ps("PB", [128, 1024], BF16)
        self.ones = S.sb("ones", [128, 128], BF16)
        S.o("dve", "memset", writes=[self.ones], ap=self.ones.t[:], constant=1.0)
        self.epsb = S.sb("epsb", [128, 1], F32)
        S.o("dve", "memset", writes=[self.epsb], ap=self.epsb.t[:], constant=EPS)
        self.identf = S.sb("identf", [128, 128], F32)
        self.identb = S.sb("identb", [128, 128], BF16)
        idd = self.din("ident", [128, 128])
        S.dma(self.identf.t[:], idd, writes=[self.identf])
        S.dma(self.identb.t[:], idd, writes=[self.identb], q="pool")

    def din(self, name, shape, dt=F32):
        if name not in self.GLOBAL:
            name = self.pre + name
        if name not in self.dram:
            self.dram[name] = self.nc.dram_tensor(name, list(shape), dt, kind="ExternalInput").ap()
        return self.dram[name]

    def phase(self):
        import contextlib

        @contextlib.contextmanager
        def cm():
            st = contextlib.ExitStack()
            self.S.stack = st
            try:
                yield
            finally:
                self.S.barrier()
                self.S.stack = None
                st.close()
        return cm()

    def dout(self, name, shape, dt=F32):
        self.dram[name] = self.nc.dram_tensor(name, list(shape), dt, kind="ExternalOutput").ap()
        return self.dram[name]


def rmsnorm_fm(pg, xt, xb, gain, hout, hb, nblk, width=512, sqbuf=None, rsbuf=None, nfeat=D_MODEL):
    S = pg.S
    KC = nfeat // 128
    sq, rs = sqbuf, rsbuf
    pn = pg.P[6]
    for b in range(nblk):
        sl = slice(b * width, (b + 1) * width)
        for kc in range(KC):
            S.o("act", "activation", [xb[b]], [sq], out=sq.t[:, kc, :width], in_=xt[:, kc, sl], func=AF.Square)
        for kc in range(KC):
            S.mm(pn.t[:, :width], pg.ones.t[:], sq.t[:, kc, :width], kc == 0, kc == KC - 1, [pg.ones, sq], [pn])
        S.o("act", "activation", [pn, pg.epsb], [rs], out=rs.t[:, :width], in_=pn.t[:, :width], func=AF.Sqrt,
            scale=1.0 / nfeat, bias=pg.epsb.t[:, 0:1])
        S.o("dve", "reciprocal", [rs], [rs], out=rs.t[:, :width], in_=rs.t[:, :width])
        for kc in range(KC):
            S.o("dve", "scalar_tensor_tensor", [xb[b], rs], [hb[b]], out=hout[:, kc, sl], in0=xt[:, kc, sl],
                scalar=gain(kc), in1=rs.t[:, :width], op0=ALU.mult, op1=ALU.mult)


def wview(W, m0, mb):
    return W.rearrange("(kc p) m -> p kc m", p=128)[:, :, m0:m0 + mb]


def ffn_pass(pg, xt, xb, ht, hb, at, ab, Wg, Wu, Wd, bufs, ntb):
    S = pg.S
    P = pg.P
    wgb, wub, wdb, sgb = bufs
    k = 0
    for j in range(NFC):
        wg, wu = wgb[j % 2], wub[j % 2]
        S.dma(wg.t[:], wview(Wg, j * 128, 128), writes=[wg], q="pool")
        S.dma(wu.t[:], wview(Wu, j * 128, 128), writes=[wu], q="pool")
        for tb in range(ntb):
            sl = slice(tb * 512, (tb + 1) * 512)
            pgt, put, sg = P[k % 2], P[2 + k % 2], sgb[k % 2]
            k += 1
            for kc in range(8):
                S.mm(pgt.t[:], wg.t[:, kc, :], ht[:, kc, sl], kc == 0, kc == 7, [wg, hb[tb]], [pgt])
            for kc in range(8):
                S.mm(put.t[:], wu.t[:, kc, :], ht[:, kc, sl], kc == 0, kc == 7, [wu, hb[tb]], [put])
            S.o("act", "activation", [pgt], [sg], out=sg.t[:], in_=pgt.t[:], func=AF.Silu)
            S.o("dve", "tensor_tensor", [sg, put], [ab[tb]], out=at[:, j, sl], in0=sg.t[:], in1=put.t[:], op=ALU.mult)
    k = 0
    Wdv = Wd.rearrange("(j p) d -> p j d", p=128)
    for dc in range(8):
        wd = wdb[dc % 2]
        S.dma(wd.t[:], Wdv[:, :, dc * 128:(dc + 1) * 128], writes=[wd], q="pool")
        for tb in range(ntb):
            sl = slice(tb * 512, (tb + 1) * 512)
            pd = P[4 + k % 2]
            k += 1
            for j in range(NFC):
                S.mm(pd.t[:], wd.t[:, j, :], at[:, j, sl], j == 0, j == NFC - 1, [wd, ab[tb]], [pd])
            S.o("dve", "scalar_tensor_tensor", [pd, xb[tb]], [xb[tb]], out=xt[:, dc, sl], in0=pd.t[:], scalar=0.5,
                in1=xt[:, dc, sl], op0=ALU.mult, op1=ALU.add)


TV = {"ffn1_norm": 0, "mix_norm": 8, "gate_bias": 16, "xattn_norm": 48, "mem_norm": 56,
      "ffn2_norm": 64, "final_norm": 72}
TV_N = 80


def post_pass(pg, d, xt, xb, ht, hb, hidt, hidb, mgt, mgb, vec, ntb, bufs):
    S = pg.S
    P = pg.P
    wgt_b, wp_b, wo_b, gs_b, tmp_b = bufs
    Wg4 = d["w_gates"].rearrange("(kc p) (br dd) -> p kc br dd", p=128, br=4)
    projs = [d["proj_a"], d["glu_w1"], d["glu_w2"], d["proj_c"], d["proj_d"]]
    src_br = [0, 1, 1, 2, 3]
    it = 0
    gk = 0
    for dc in range(8):
        wgt, wp = wgt_b[dc % 2], wp_b[dc % 2]
        for br in range(4):
            S.dma(wgt.t[:, :, br, :], Wg4[:, :, br, dc * 128:(dc + 1) * 128], writes=[wgt], q="pool")
        for i, pw in enumerate(projs):
            S.dma(wp.t[:, :, i, :], wview(pw, dc * 128, 128), writes=[wp], q="pool")
        for tb in range(ntb):
            sl = slice(tb * 512, (tb + 1) * 512)
            gs = gs_b[it % 2]
            t0, t1 = tmp_b[(it % 2) * 2], tmp_b[(it % 2) * 2 + 1]
            it += 1
            for br in range(4):
                pgt = P[5 + gk % 2]
                gk += 1
                for kc in range(8):
                    S.mm(pgt.t[:], wgt.t[:, kc, br, :], ht[:, kc, sl], kc == 0, kc == 7, [wgt, hb[tb]], [pgt])
                S.o("act", "activation", [pgt, vec], [gs[br]], out=gs[br].t[:], in_=pgt.t[:], func=AF.Sigmoid,
                    bias=vec.t[:, TV["gate_bias"] + br * 8 + dc:TV["gate_bias"] + br * 8 + dc + 1])
            ys = []
            for i in (2, 1, 0, 3, 4):
                py = P[i]
                for kc in range(4):
                    S.mm(py.t[:], wp.t[:, kc, i, :], hidt[:, src_br[i] * 4 + kc, sl], kc == 0, kc == 3,
                         [wp, hidb[tb]], [py])
                if i == 2:
                    S.o("act", "activation", [py], [t1], out=t1.t[:], in_=py.t[:], func=AF.Sigmoid)
            ys = [P[0], P[1], P[2], P[3], P[4]]
            S.o("dve", "tensor_tensor", [t1, ys[1]], [t1], out=t1.t[:], in0=t1.t[:], in1=ys[1].t[:], op=ALU.mult)
            S.o("dve", "tensor_tensor", [t1, gs[1]], [t1], out=t1.t[:], in0=t1.t[:], in1=gs[1].t[:], op=ALU.mult)
            S.o("dve", "tensor_tensor", [gs[0], ys[0]], [t0], out=t0.t[:], in0=gs[0].t[:], in1=ys[0].t[:], op=ALU.mult)
            S.o("dve", "tensor_tensor", [t0, t1], [t0], out=t0.t[:], in0=t0.t[:], in1=t1.t[:], op=ALU.add)
            S.o("dve", "tensor_tensor", [gs[2], ys[3]], [t1], out=t1.t[:], in0=gs[2].t[:], in1=ys[3].t[:], op=ALU.mult)
            S.o("dve", "tensor_tensor", [t0, t1], [t0], out=t0.t[:], in0=t0.t[:], in1=t1.t[:], op=ALU.add)
            S.o("dve", "tensor_tensor", [gs[3], ys[4]], [t1], out=t1.t[:], in0=gs[3].t[:], in1=ys[4].t[:], op=ALU.mult)
            S.o("dve", "tensor_tensor", [t0, t1], [mgb[tb]], out=mgt[:, dc, sl], in0=t0.t[:], in1=t1.t[:], op=ALU.add)
    linear_res(pg, d["w_out"], xt, xb, mgt, mgb, ntb, wo_b, 1.0)


def linear_res(pg, W, xt, xb, it, ib, ntb, wbufs, alpha):
    S = pg.S
    k = 0
    for dc in range(8):
        wo = wbufs[dc % 2]
        S.dma(wo.t[:], wview(W, dc * 128, 128), writes=[wo], q="pool")
        for tb in range(ntb):
            sl = slice(tb * 512, (tb + 1) * 512)
            po = pg.P[4 + k % 2]
            k += 1
            for kc in range(8):
                S.mm(po.t[:], wo.t[:, kc, :], it[:, kc, sl], kc == 0, kc == 7, [wo, ib[tb]], [po])
            S.o("dve", "scalar_tensor_tensor", [po, xb[tb]], [xb[tb]], out=xt[:, dc, sl], in0=po.t[:], scalar=alpha,
                in1=xt[:, dc, sl], op0=ALU.mult, op1=ALU.add)


class DView:
    def __init__(self, pg):
        self.pg = pg

    def __getitem__(self, k):
        return self.pg.dram[self.pg.pre + k]


def xv(ap):
    return ap.rearrange("(kc p) n -> p kc n", p=128)


def tok_phase(pg, do_post, do_pre, do_final, NT, TP, io):
    S = pg.S
    ntb = TP // 512
    vecd = pg.din("vecs", [128, TV_N])
    d = DView(pg)
    if do_post:
        pg.din("w_gates", [1024, 4096])
        for n in ("proj_a", "glu_w1", "glu_w2", "proj_c", "proj_d"):
            pg.din(n, [512, 1024])
        for n in ("w_out", "wq", "wk", "wv", "wo"):
            pg.din(n, [1024, 1024])
        pg.din("f2_wg", [1024, D_FF]); pg.din("f2_wu", [1024, D_FF]); pg.din("f2_wd", [D_FF, 1024])
    if do_pre:
        pg.din("f1_wg", [1024, D_FF]); pg.din("f1_wu", [1024, D_FF]); pg.din("f1_wd", [D_FF, 1024])
    vec = S.sb("vec", [128, TV_N], F32)
    S.dma(vec.t[:], vecd, writes=[vec])
    x = S.sb("x", [128, 8, TP], F32)
    xb = [Buf(x.t) for _ in range(ntb)]
    a = S.sb("a", [128, NFC, TP], BF16)
    ab = [Buf(a.t) for _ in range(ntb)]
    G0 = S.sb("G0", [128, 8, TP], BF16)
    g0b = [Buf(G0.t) for _ in range(ntb)]
    G1 = S.sb("G1", [128, 8, TP], BF16)
    g1b = [Buf(G1.t) for _ in range(ntb)]
    wgb = [S.sb("wg", [128, 8, 128], BF16) for _ in range(2)]
    wub = [S.sb("wu", [128, 8, 128], BF16) for _ in range(2)]
    wdb = [S.sb("wd", [128, NFC, 128], BF16) for _ in range(2)]
    tmpb = [S.sb("tmp", [128, 512], F32) for _ in range(4 if do_post else 2)]
    sq = S.sb("sq", [128, 8, 512], BF16)
    rs = S.sb("rs", [128, 512], F32)
    ffn_bufs = (wgb, wub, wdb, tmpb)
    if do_post:
        wgt_b = [S.sb("wgt", [128, 8, 4, 128], BF16) for _ in range(2)]
        wp_b = [S.sb("wp", [128, 4, 5, 128], BF16) for _ in range(2)]
        gs_b = [[S.sb("gs", [128, 512], BF16) for _ in range(4)] for _ in range(2)]
        mT = Buf(a.t[:, 16:20, :].bitcast(F32).rearrange("p a (b c) -> p (a b) c", b=2))
        mn = S.sb("mn", [128, 8, 256], BF16)
        KT = S.sb("KT", [128, 8, 256], BF16)
        Vm = S.sb("Vm", [128, 2, 1024], BF16)
        pf = [S.sb("pf", [128, 256], F32) for _ in range(2)]
        pb = [S.sb("pb", [128, 256], BF16) for _ in range(2)]
        st = [S.sb("st", [128, 4], F32) for _ in range(2)]
    for pi in range(NT // TP):
        tok0 = pi * TP
        for tb in range(ntb):
            sl = slice(tb * 512, (tb + 1) * 512)
            S.dma(x.t[:, :, sl], io["x_src"](tok0 + tb * 512, tok0 + (tb + 1) * 512), writes=[xb[tb]])
        if do_post:
            for tb in range(ntb):
                sl = slice(tb * 512, (tb + 1) * 512)
                g0, g1 = tok0 + tb * 512, tok0 + (tb + 1) * 512
                S.dma(G0.t[:, :, sl], io["h_src"](g0, g1), writes=[g0b[tb]])
                io["hid_load"](S, a.t, ab[tb], sl, g0, g1)
            post_pass(pg, d, x.t, xb, G0.t, g0b, a.t, ab, G1.t, g1b, vec, ntb,
                      (wgt_b, wp_b, wgb, gs_b, tmpb))
            if pi == 0:
                wk = Buf(a.t[:, 0:8, :]); wv = Buf(a.t[:, 8:16, :])
                xattn_kv_arena(pg, d, vec, KT, Vm, mT, gs_b, mn, sq, rs, wk, wv, ab, io["memT"])
            PT = a.t[:, 0:8, :].rearrange("p (h mc) t -> p h mc t", h=4)
            qt = a.t[:, 8:16, :]
            xattn_pass_arena(pg, d, x.t, xb, G1.t, g1b, qt, G0.t, g0b, vec, KT, Vm, ntb,
                             (wgb, sq, rs, pf, pb, st, PT), ab)
            rmsnorm_fm(pg, x.t, xb, lambda kc: vec.t[:, TV["ffn2_norm"] + kc:TV["ffn2_norm"] + kc + 1], G0.t, g0b, ntb,
                       sqbuf=sq, rsbuf=rs)
            ffn_pass(pg, x.t, xb, G0.t, g0b, a.t, ab, d["f2_wg"], d["f2_wu"], d["f2_wd"], ffn_bufs, ntb)
        if do_pre:
            rmsnorm_fm(pg, x.t, xb, lambda kc: vec.t[:, TV["ffn1_norm"] + kc:TV["ffn1_norm"] + kc + 1], G0.t, g0b, ntb,
                       sqbuf=sq, rsbuf=rs)
            ffn_pass(pg, x.t, xb, G0.t, g0b, a.t, ab, d["f1_wg"], d["f1_wu"], d["f1_wd"], ffn_bufs, ntb)
            rmsnorm_fm(pg, x.t, xb, lambda kc: vec.t[:, TV["mix_norm"] + kc:TV["mix_norm"] + kc + 1], G0.t, g0b, ntb,
                       sqbuf=sq, rsbuf=rs)
            for tb in range(ntb):
                sl = slice(tb * 512, (tb + 1) * 512)
                g0, g1 = tok0 + tb * 512, tok0 + (tb + 1) * 512
                for f in io["h_dst"]:
                    S.dma(f(g0, g1), G0.t[:, :, sl], reads=[g0b[tb]])
                for f in io["x_dst"]:
                    S.dma(f(g0, g1), x.t[:, :, sl], reads=[xb[tb]])
        if do_final:
            rmsnorm_fm(pg, x.t, xb, lambda kc: vec.t[:, TV["final_norm"] + kc:TV["final_norm"] + kc + 1], x.t, xb, ntb,
                       sqbuf=sq, rsbuf=rs)
            for tb in range(ntb):
                sl = slice(tb * 512, (tb + 1) * 512)
                g0, g1 = tok0 + tb * 512, tok0 + (tb + 1) * 512
                S.dma(io["y_dst"](g0, g1), x.t[:, :, sl], reads=[xb[tb]])


def build_tok(do_post, do_pre, do_final, NT=2048, TP=1024):
    pg = Prog()
    xT = pg.din("xT", [1024, NT])
    io = {"x_src": lambda g0, g1: xv(xT)[:, :, g0:g1]}
    if do_post:
        hT = pg.din("hT", [1024, NT], BF16)
        hid = pg.din("hid", [2048, NT], BF16)
        io["memT"] = pg.din("memT", [1024, 256])
        io["h_src"] = lambda g0, g1: xv(hT)[:, :, g0:g1]
        io["hid_load"] = lambda S, at, abuf, sl, g0, g1: S.dma(at[:, 0:16, sl], xv(hid)[:, :, g0:g1], writes=[abuf])
    if do_pre:
        xo = pg.dout("xT_out", [1024, NT])
        ho = pg.dout("hT_out", [1024, NT], BF16)
        io["x_dst"] = [lambda g0, g1: xv(xo)[:, :, g0:g1]]
        io["h_dst"] = [lambda g0, g1: xv(ho)[:, :, g0:g1]]
    if do_final:
        yo = pg.dout("yT", [1024, NT])
        io["y_dst"] = lambda g0, g1: xv(yo)[:, :, g0:g1]
    with pg.phase():
        tok_phase(pg, do_post, do_pre, do_final, NT, TP, io)
    pg.S.finish()
    return pg


def xattn_kv_arena(pg, d, vec, KT, Vm, mT, gs_b, mn, sq, rs, wk, wv, ab, memT):
    S = pg.S
    P = pg.P
    S.dma(mT.t, memT.rearrange("(kc p) m -> p kc m", p=128), writes=ab)
    rmsnorm_fm(pg, mT.t, ab, lambda kc: vec.t[:, TV["mem_norm"] + kc:TV["mem_norm"] + kc + 1], mn.t, [mn], 1,
               width=256, sqbuf=sq, rsbuf=rs)
    S.dma(wk.t, wview(d["wk"], 0, 1024), writes=ab, q="pool")
    S.dma(wv.t, wview(d["wv"], 0, 1024), writes=ab, q="pool")
    for dc in range(8):
        pk = P[dc % 2]
        for kc in range(8):
            S.mm(pk.t[:, :256], wk.t[:, kc, dc * 128:(dc + 1) * 128], mn.t[:, kc, :], kc == 0, kc == 7, ab + [mn], [pk])
        S.o("act", "activation", [pk], [KT], out=KT.t[:, dc, :], in_=pk.t[:, :256], func=AF.Copy)
    k = 0
    for mc in range(2):
        for hf in range(2):
            pv = P[2 + k % 2]
            k += 1
            for kc in range(8):
                S.mm(pv.t[:], mn.t[:, kc, mc * 128:(mc + 1) * 128], wv.t[:, kc, hf * 512:(hf + 1) * 512], kc == 0,
                     kc == 7, ab + [mn], [pv])
            S.o("act", "activation", [pv], [Vm], out=Vm.t[:, mc, hf * 512:(hf + 1) * 512], in_=pv.t[:], func=AF.Copy)


def xattn_pass_arena(pg, d, xt, xb, hxt, hxb, qt, ot, ob, vec, KT, Vm, ntb, bufs, ab):
    S = pg.S
    P = pg.P
    wq_b, sq, rs, pf_l, pb_l, st_l, PT = bufs
    TP = ntb * 512
    rmsnorm_fm(pg, xt, xb, lambda kc: vec.t[:, TV["xattn_norm"] + kc:TV["xattn_norm"] + kc + 1], hxt, hxb, ntb,
               sqbuf=sq, rsbuf=rs)
    k = 0
    for dc in range(8):
        wq = wq_b[dc % 2]
        S.dma(wq.t[:], wview(d["wq"], dc * 128, 128), writes=[wq], q="pool")
        for tb in range(ntb):
            sl = slice(tb * 512, (tb + 1) * 512)
            pq = P[k % 2]
            k += 1
            for kc in range(8):
                S.mm(pq.t[:], wq.t[:, kc, :], hxt[:, kc, sl], kc == 0, kc == 7, [wq, hxb[tb]], [pq])
            S.o("act", "activation", [pq], ab, out=qt[:, dc, sl], in_=pq.t[:], func=AF.Copy, scale=1.0 / 16.0)
    k = 0
    for hd in range(4):
        for tc in range(TP // 128):
            tsl = slice(tc * 128, (tc + 1) * 128)
            psc = P[2 + k % 2]
            pf, pb, st = pf_l[k % 2], pb_l[k % 2], st_l[k % 2]
            k += 1
            for dcl in range(2):
                S.mm(psc.t[:, :256], qt[:, hd * 2 + dcl, tsl], KT.t[:, hd * 2 + dcl, :], dcl == 0, dcl == 1,
                     ab + [KT], [psc])
            S.o("dve", "reduce_max", [psc], [st], out=st.t[:, 0:1], in_=psc.t[:, :256], axis=AX.X)
            S.o("dve", "tensor_scalar", [st], [st], out=st.t[:, 1:2], in0=st.t[:, 0:1], scalar1=-1.0, scalar2=None,
                op0=ALU.mult)
            S.o("act", "activation", [psc, st], [pf, st], out=pf.t[:], in_=psc.t[:, :256], func=AF.Exp,
                bias=st.t[:, 1:2], accum_out=st.t[:, 2:3])
            S.o("dve", "reciprocal", [st], [st], out=st.t[:, 3:4], in_=st.t[:, 2:3])
            S.o("dve", "tensor_scalar", [pf, st], [pb], out=pb.t[:], in0=pf.t[:], scalar1=st.t[:, 3:4], scalar2=None,
                op0=ALU.mult)
            for mc in range(2):
                S.o("pe", "transpose", [pb, pg.identb], [pg.PB], out=pg.PB.t[:, mc * 128:(mc + 1) * 128],
                    in_=pb.t[:, mc * 128:(mc + 1) * 128], identity=pg.identb.t[:])
            S.o("act", "activation", [pg.PB], ab, out=PT[:, hd, :, tsl],
                in_=pg.PB.t[:, 0:256].rearrange("p (mc t) -> p mc t", mc=2), func=AF.Copy)
    k = 0
    for dc in range(8):
        hd = dc // 2
        for tb in range(ntb):
            sl = slice(tb * 512, (tb + 1) * 512)
            po = P[k % 2]
            k += 1
            for mc in range(2):
                S.mm(po.t[:], Vm.t[:, mc, dc * 128:(dc + 1) * 128], PT[:, hd, mc, sl], mc == 0, mc == 1, ab + [Vm], [po])
            S.o("act", "activation", [po], [ob[tb]], out=ot[:, dc, sl], in_=po.t[:], func=AF.Copy)
    linear_res(pg, d["wo"], xt, xb, ot, ob, ntb, wq_b, 1.0)


IDENT = np.eye(128, dtype=np.float32)
IN_W = (512, 512, 4, 4, 512, 256, 256, 512, 512, 16, 1792, 4096)
IN_OFF = np.concatenate([[0], np.cumsum(IN_W)]).astype(int)


def tok_weights(p, l_post, l_pre, final):
    m = {"ident": IDENT}
    vecs = np.zeros((128, TV_N), np.float32)
    if l_post is not None:
        l = l_post
        m["w_gates"] = np.ascontiguousarray(p["w_in"][l][:, IN_OFF[11]:IN_OFF[12]])
        m["proj_a"] = p["mlstm_proj"][l]; m["glu_w1"] = p["s5_glu_w1"][l]; m["glu_w2"] = p["s5_glu_w2"][l]
        m["proj_c"] = p["gla_proj"][l]; m["proj_d"] = p["rwkv_proj"][l]
        m["w_out"] = p["w_out"][l]
        m["wq"] = p["xattn_wq"][l]; m["wk"] = p["xattn_wk"][l]; m["wv"] = p["xattn_wv"][l]; m["wo"] = p["xattn_wo"][l]
        m["f2_wg"] = p["ffn2_w_gate"][l]; m["f2_wu"] = p["ffn2_w_up"][l]; m["f2_wd"] = p["ffn2_w_down"][l]
        vecs[:, TV["gate_bias"]:TV["gate_bias"] + 32] = vec_fm(p["gate_bias"][l])
        vecs[:, TV["xattn_norm"]:TV["xattn_norm"] + 8] = vec_fm(p["xattn_norm"][l])
        vecs[:, TV["mem_norm"]:TV["mem_norm"] + 8] = vec_fm(p["mem_norm"][l])
        vecs[:, TV["ffn2_norm"]:TV["ffn2_norm"] + 8] = vec_fm(p["ffn2_norm"][l])
    if l_pre is not None:
        l = l_pre
        m["f1_wg"] = p["ffn1_w_gate"][l]; m["f1_wu"] = p["ffn1_w_up"][l]; m["f1_wd"] = p["ffn1_w_down"][l]
        vecs[:, TV["ffn1_norm"]:TV["ffn1_norm"] + 8] = vec_fm(p["ffn1_norm"][l])
        vecs[:, TV["mix_norm"]:TV["mix_norm"] + 8] = vec_fm(p["mix_norm"][l])
    if final:
        vecs[:, TV["final_norm"]:TV["final_norm"] + 8] = vec_fm(p["final_norm"])
    m["vecs"] = vecs
    return {k: np.ascontiguousarray(v, dtype=np.float32) for k, v in m.items()}


MV = {"ml": 0,
      "s5": 16,
      "gla": 24,
      "rw": 32}
MV_N = 96
SEG = 512


def consts_np():
    c = {}
    s = np.arange(128)
    c["uincl"] = (s[:, None] <= s[None, :]).astype(np.float32)
    c["ustrict"] = (s[:, None] < s[None, :]).astype(np.float32)
    c["lstrict"] = (s[:, None] > s[None, :]).astype(np.float32)
    t = np.arange(SEG)
    c["rm64"] = np.broadcast_to((t % 64 != 0).astype(np.float32), (128, SEG)).copy()
    c["rm128"] = np.broadcast_to((t % 128 != 0).astype(np.float32), (128, SEG)).copy()
    bo = np.zeros((128, 128), np.float32)
    bo[:64, :64] = 1.0
    bo[64:, 64:] = 1.0
    c["bones"] = bo
    return c


class Mix:
    def __init__(self, pg, T, hsrc, hid_dst):
        self.pg = pg
        self.nc, self.S, self.P, self.PB = pg.nc, pg.S, pg.P, pg.PB
        self.ones, self.epsb, self.identf, self.identb = pg.ones, pg.epsb, pg.identf, pg.identb
        self.dram = pg.dram
        S = self.S
        self.T = T
        self.hsrc, self.hid_dst = hsrc, hid_dst
        self.vec = S.sb("mvec", [128, MV_N], F32)
        S.dma(self.vec.t[:], self.din("mvecs", [128, MV_N]), writes=[self.vec])
        self.c = {}
        for n, shp in (("uincl", [128, 128]), ("ustrict", [128, 128]), ("lstrict", [128, 128]),
                       ("rm64", [128, SEG]), ("rm128", [128, SEG]), ("bones", [128, 128])):
            b = S.sb(n, shp, F32)
            S.dma(b.t[:], self.din(n, shp), writes=[b])
            self.c[n] = b
        self.bonesb = S.sb("bonesb", [128, 128], BF16)
        S.dma(self.bonesb.t[:], self.dram["bones"], writes=[self.bonesb], q="pool")
        self.one1 = S.sb("one1", [128, 1], F32)
        S.o("dve", "memset", writes=[self.one1], ap=self.one1.t[:], constant=1.0)
        self.hseg = [S.sb("hseg", [128, 8, SEG], BF16) for _ in range(2)]
        self.F = [S.sb("F", [128, SEG], F32) for _ in range(10)]
        self.B = [S.sb("Bb", [128, SEG], BF16) for _ in range(8)]

    def din(self, name, shape, dt=F32):
        return self.pg.din(name, shape, dt)

    def v(self, col, n=1):
        return self.vec.t[:, col:col + n]

    def proj(self, wt, c0, M, hs, pbuf):
        S = self.S
        for kc in range(8):
            S.mm(pbuf.t[:M, :], wt.t[:, kc, c0:c0 + M], hs.t[:, kc, :], kc == 0, kc == 7, [wt, hs], [pbuf])
        return pbuf

    def headnorm(self, hh_f, nparts_mean, gain_ap, gate_f, outb, pbuf, sqb, rsf, eps=EPS, ones=None, center=False):
        S = self.S
        ones = ones or self.ones
        S.o("act", "activation", [hh_f], [sqb], out=sqb.t[:], in_=hh_f.t[:], func=AF.Square)
        S.mm(pbuf.t[:], ones.t[:], sqb.t[:], True, True, [ones, sqb], [pbuf])
        S.o("act", "activation", [pbuf, self.epsb], [rsf], out=rsf.t[:], in_=pbuf.t[:], func=AF.Sqrt,
            scale=1.0 / nparts_mean, bias=self.epsb.t[:, 0:1] if eps == EPS else eps)
        S.o("dve", "reciprocal", [rsf], [rsf], out=rsf.t[:], in_=rsf.t[:])
        S.o("dve", "scalar_tensor_tensor", [hh_f, rsf, self.vec], [rsf], out=rsf.t[:], in0=hh_f.t[:], scalar=gain_ap,
            in1=rsf.t[:], op0=ALU.mult, op1=ALU.mult)
        S.o("dve", "tensor_tensor", [rsf, gate_f], [outb], out=outb.t[:], in0=rsf.t[:], in1=gate_f.t[:], op=ALU.mult)


def mlstm_setup(mx):
    S = mx.S
    st = {}
    st["w"] = S.sb("mlw", [128, 8, 1024], BF16)
    S.dma(st["w"].t[:], wview(mx.din("ml_w", [1024, 1024]), 0, 1024), writes=[st["w"]], q="pool")
    st["qkv"] = S.sb("mlqkv", [128, 2, 3, 128], BF16)
    S.dma(st["qkv"].t[:], mx.din("ml_qkv", [128, 2, 3, 128]), writes=[st["qkv"]], q="pool")
    st["C"] = [S.sb("mlC", [128, 256], F32) for _ in range(2)]
    st["Cb"] = [S.sb("mlCb", [128, 256], BF16) for _ in range(2)]
    st["upad"] = [S.sb("upad", [128, SEG + 3], F32) for _ in range(2)]
    st["Vx"] = [S.sb("mlVx", [64, 8, 256], BF16) for _ in range(2)]
    for b in st["Vx"]:
        S.o("dve", "memset", writes=[b], ap=b.t[:], constant=1.0)
    st["ktm"] = [S.sb("ktm", [64, 128], BF16) for _ in range(2)]
    st["Wt"] = [S.sb("Wt", [64, 64], BF16) for _ in range(2)]
    st["F"] = [[S.sb("mlF", [128, SEG], F32) for _ in range(9)] for _ in range(2)]
    st["B"] = [[S.sb("mlB", [128, SEG], BF16) for _ in range(6)] for _ in range(2)]
    st["nbf"] = S.sb("nbf", [128, 2], F32)
    for hh in range(2):
        S.o("dve", "memset", writes=[st["C"][hh]], ap=st["C"][hh].t[:], constant=0.0)
        S.o("dve", "memset", writes=[st["Cb"][hh]], ap=st["Cb"][hh].t[:], constant=0.0)
        S.o("dve", "memset", writes=[st["upad"][hh]], ap=st["upad"][hh].t[:], constant=0.0)
        S.o("dve", "tensor_scalar", [mx.vec], [st["nbf"]], out=st["nbf"].t[:, hh:hh + 1],
            in0=mx.v(MV["ml"] + hh * 8 + 5), scalar1=-1.0, scalar2=None, op0=ALU.mult)
    return st


import os
DBG = int(os.environ.get('DBG', '99'))


def mlstm_seg(mx, st, hs, sg):
    S = mx.S
    P = mx.P
    w = st["w"]
    m64 = mx.c["uincl"]
    hd_ = []
    for hh in range(2):
        F, B = st["F"][hh], st["B"][hh]
        vb = MV["ml"] + hh * 8
        c0 = hh * 512
        upad, C, Cb, Vx = st["upad"][hh], st["C"][hh], st["Cb"][hh], st["Vx"][hh]
        PA, PBk = P[2 * hh], P[2 * hh + 1]
        mx.proj(w, c0, 128, hs, PA)
        S.o("act", "activation", [PA], [upad], out=upad.t[:, 3:SEG + 3], in_=PA.t[:], func=AF.Copy)
        ub = B[0]
        S.o("dve", "tensor_copy", [upad], [ub], out=ub.t[:], in_=upad.t[:, 3:SEG + 3])
        acc = F[0]
        S.o("dve", "tensor_scalar", [upad, mx.vec], [acc], out=acc.t[:], in0=upad.t[:, 0:SEG], scalar1=mx.v(vb + 0),
            scalar2=None, op0=ALU.mult)
        for j in range(1, 4):
            S.o("dve", "scalar_tensor_tensor", [upad, mx.vec, acc], [acc], out=acc.t[:], in0=upad.t[:, j:SEG + j],
                scalar=mx.v(vb + j), in1=acc.t[:], op0=ALU.mult, op1=ALU.add)
        S.o("dve", "tensor_copy", [upad], [upad], out=upad.t[:, 0:3], in_=upad.t[:, SEG:SEG + 3])
        uc = B[1]
        S.o("act", "activation", [acc], [uc], out=uc.t[:], in_=acc.t[:], func=AF.Silu)
        mx.proj(w, c0 + 256, 128, hs, PA)
        mx.proj(w, c0 + 384, 128, hs, PBk)
        e1, b, eb, ek = F[1], F[2], F[3], F[4]
        S.o("act", "activation", [PBk, st["nbf"]], [e1], out=e1.t[:], in_=PBk.t[:], func=AF.Exp, scale=-1.0,
            bias=st["nbf"].t[:, hh:hh + 1])
        S.o("act", "activation", [e1, mx.one1], [e1], out=e1.t[:], in_=e1.t[:], func=AF.Ln, bias=mx.one1.t[:, 0:1])
        S.o("dve", "tensor_scalar", [e1], [e1], out=e1.t[:], in0=e1.t[:], scalar1=-1.0, scalar2=None, op0=ALU.mult)
        S.o("dve", "tensor_tensor_scan", [mx.c["rm64"], e1], [b], out=b.t[:], data0=mx.c["rm64"].t[:], data1=e1.t[:],
            initial=0.0, op0=ALU.mult, op1=ALU.add)
        S.o("act", "activation", [b], [eb], out=eb.t[:], in_=b.t[:], func=AF.Exp)
        S.o("dve", "scalar_tensor_tensor", [PA, mx.vec, b], [ek], out=ek.t[:], in0=PA.t[:], scalar=mx.v(vb + 4),
            in1=b.t[:], op0=ALU.add, op1=ALU.subtract)
        S.o("act", "activation", [ek], [ek], out=ek.t[:], in_=ek.t[:], func=AF.Exp)
        S.mm(PA.t[:], st["qkv"].t[:, hh, 0, :], uc.t[:], True, True, [st["qkv"], uc], [PA])
        S.mm(PBk.t[:], st["qkv"].t[:, hh, 1, :], uc.t[:], True, True, [st["qkv"], uc], [PBk])
        qt, kt = B[2], B[3]
        S.o("dve", "tensor_tensor", [PA, eb], [qt], out=qt.t[:], in0=PA.t[:], in1=eb.t[:], op=ALU.mult)
        S.o("dve", "scalar_tensor_tensor", [PBk, ek], [kt], out=kt.t[:], in0=PBk.t[:], scalar=128.0 ** -0.5,
            in1=ek.t[:], op0=ALU.mult, op1=ALU.mult)
        for half in range(2):
            pv = P[4 + half]
            for cc in range(4):
                c = half * 4 + cc
                S.mm(pv.t[0:64, cc * 128:(cc + 1) * 128], ub.t[:, c * 64:(c + 1) * 64], st["qkv"].t[:, hh, 2, :], True, True,
                     [ub, st["qkv"]], [pv])
            S.o("act", "activation", [pv], [Vx], out=Vx.t[:, half * 4:half * 4 + 4, 0:128],
                in_=pv.t[0:64, :].rearrange("p (c v) -> p c v", c=4), func=AF.Copy)
        hd_.append(dict(F=F, B=B, vb=vb, c0=c0, C=C, Cb=Cb, Vx=Vx, PN=PA, PD=PBk, qt=qt, kt=kt, eb=eb,
                        ktm=st["ktm"][hh], Wt=st["Wt"][hh]))
    PSC, PS_ = P[4], P[5]
    for c in range(8):
        cs = slice(c * 64, (c + 1) * 64)
        for hh in range(2):
            h = hd_[hh]
            kt, qt, Vx, Cb, C, eb, PN, PD, ktm, Wt = (h[k] for k in ("kt", "qt", "Vx", "Cb", "C", "eb", "PN", "PD", "ktm", "Wt"))
            S.o("pe", "transpose", [kt, mx.identb], [mx.PB], out=mx.PB.t[0:64, 0:128], in_=kt.t[:, cs],
                identity=mx.identb.t[:])
            S.o("act", "activation", [mx.PB], [ktm], out=ktm.t[:], in_=mx.PB.t[0:64, 0:128], func=AF.Copy)
            S.mm(PSC.t[0:64, 0:64], kt.t[:, cs], qt.t[:, cs], True, True, [kt, qt], [PSC])
            S.o("dve", "tensor_tensor", [PSC, m64], [Wt], out=Wt.t[:], in0=PSC.t[0:64, 0:64],
                in1=m64.t[0:64, 0:64], op=ALU.mult)
            S.mm(PN.t[:, cs], Vx.t[:, c, 0:128], Wt.t[:], True, False, [Vx, Wt], [PN])
            S.mm(PN.t[:, cs], Cb.t[:, 0:128], qt.t[:, cs], False, True, [Cb, qt], [PN])
            S.mm(PD.t[:, cs], Vx.t[:, c, 128:256], Wt.t[:], True, False, [Vx, Wt], [PD])
            S.mm(PD.t[:, cs], Cb.t[:, 128:256], qt.t[:, cs], False, True, [Cb, qt], [PD])
            S.mm(PS_.t[:, 0:256], ktm.t[:], Vx.t[:, c, :], True, True, [ktm, Vx], [PS_])
            ebl = eb.t[:, c * 64 + 63:c * 64 + 64]
            S.o("dve", "tensor_scalar", [C, eb], [C], out=C.t[:], in0=C.t[:], scalar1=ebl, scalar2=None, op0=ALU.mult)
            S.o("dve", "scalar_tensor_tensor", [PS_, eb, C], [C], out=C.t[:], in0=PS_.t[:, 0:256], scalar=ebl,
                in1=C.t[:], op0=ALU.mult, op1=ALU.add)
            S.o("act", "activation", [C], [Cb], out=Cb.t[:], in_=C.t[:], func=AF.Copy)
    for hh in range(2):
        h = hd_[hh]
        F, B, PN, PD = h["F"], h["B"], h["PN"], h["PD"]
        dd, hh_f = F[5], F[6]
        S.o("act", "activation", [PD], [dd], out=dd.t[:], in_=PD.t[:], func=AF.Abs)
        S.o("dve", "tensor_scalar", [dd], [dd], out=dd.t[:], in0=dd.t[:], scalar1=1.0, scalar2=None, op0=ALU.max)
        S.o("dve", "reciprocal", [dd], [dd], out=dd.t[:], in_=dd.t[:])
        S.o("dve", "tensor_tensor", [PN, dd], [hh_f], out=hh_f.t[:], in0=PN.t[:], in1=dd.t[:], op=ALU.mult)
        mx.proj(w, h["c0"] + 128, 128, hs, P[4 + hh])
        og = F[7]
        S.o("act", "activation", [P[4 + hh]], [og], out=og.t[:], in_=P[4 + hh].t[:], func=AF.Sigmoid)
        ob = B[4]
        mx.headnorm(hh_f, 128, mx.v(h["vb"] + 6), og, ob, P[6], B[5], F[8])
        S.dma(mx.hid_dst(hh * 128, sg), ob.t[:], reads=[ob])


def mix_phase(pg, T, do, hsrc, hid_dst):
    mx = Mix(pg, T, hsrc, hid_dst)
    S = mx.S
    sts = {}
    if "ml" in do:
        sts["ml"] = mlstm_setup(mx)
    if "gla" in do:
        sts["gla"] = gla_setup(mx)
    if "s5" in do:
        sts["s5"] = s5_setup(mx)
    if "rw" in do:
        sts["rw"] = rwkv_setup(mx)
    for sg in range(T // SEG):
        hs = mx.hseg[sg % 2]
        S.dma(hs.t[:], hsrc(sg), writes=[hs])
        if "ml" in do:
            mlstm_seg(mx, sts["ml"], hs, sg)
        if "s5" in do:
            s5_seg(mx, sts["s5"], hs, sg)
        if "gla" in do:
            gla_seg(mx, sts["gla"], hs, sg)
        if "rw" in do:
            rwkv_seg(mx, sts["rw"], hs, sg)
    return mx


def build_mix(T=4096, do=("ml", "s5", "gla", "rw")):
    pg = Prog()
    hT = pg.din("hT", [1024, T], BF16)
    hid = pg.dout("hid", [1024, T], BF16)
    with pg.phase():
        mix_phase(pg, T, do, lambda sg: xv(hT)[:, :, sg * SEG:(sg + 1) * SEG],
                  lambda row0, sg: hid[row0:row0 + 128, sg * SEG:(sg + 1) * SEG])
    pg.S.finish()
    return pg


def mix_inputs(p, l, j, hT):
    m = {"ident": IDENT, "hT": hT}
    m.update(consts_np())
    win = p["w_in"][l]
    mv = np.zeros((128, MV_N), np.float32)
    rep = lambda col: np.repeat(win[:, col:col + 1], 128, axis=1)
    cols = []
    for hh in range(2):
        h = 2 * j + hh
        cols += [win[:, IN_OFF[0] + h * 128: IN_OFF[0] + (h + 1) * 128], win[:, IN_OFF[1] + h * 128: IN_OFF[1] + (h + 1) * 128],
                 rep(IN_OFF[2] + h), rep(IN_OFF[3] + h)]
        vb = MV["ml"] + hh * 8
        mv[:, vb:vb + 4] = p["mlstm_conv"][l][:, h * 128:(h + 1) * 128].T
        mv[:, vb + 4] = p["mlstm_b_i"][l][h]
        mv[:, vb + 5] = p["mlstm_b_f"][l][h]
        mv[:, vb + 6] = p["mlstm_norm"][l][h * 128:(h + 1) * 128]
    m["ml_w"] = np.concatenate(cols, axis=1)
    qkv = np.stack([np.stack([p["mlstm_wq"][l][2 * j + hh], p["mlstm_wk"][l][2 * j + hh], p["mlstm_wv"][l][2 * j + hh]], 0)
                    for hh in range(2)], 0)
    m["ml_qkv"] = np.ascontiguousarray(qkv.transpose(2, 0, 1, 3))
    hq = slice(j * 128, (j + 1) * 128)
    hv = slice(j * 256, (j + 1) * 256)
    m["gla_w"] = np.concatenate([win[:, IN_OFF[5]:IN_OFF[6]][:, hq], win[:, IN_OFF[6]:IN_OFF[7]][:, hq],
                                 win[:, IN_OFF[8]:IN_OFF[9]][:, hv], win[:, IN_OFF[7]:IN_OFF[8]][:, hv],
                                 win[:, IN_OFF[9]:IN_OFF[10]], np.zeros((1024, 112), np.float32)], axis=1)
    m["gla_aup"] = np.concatenate([p["gla_a_up"][l][:, hq], np.zeros((112, 128), np.float32)], axis=0)
    mv[:, MV["gla"]] = p["gla_a_bias"][l][hq]
    mv[:, MV["gla"] + 1:MV["gla"] + 3] = vec_fm(p["gla_norm"][l][hv])
    m["s5_w"] = win[:, IN_OFF[4]:IN_OFF[5]][:, hv]
    sB = np.zeros((128, 8, 2, 128), np.float32)
    sC = np.zeros((128, 8, 2, 128), np.float32)
    par = np.zeros((128, 3, 8), np.float32)
    for mt in range(8):
        for gi in range(2):
            g = j * 16 + mt * 2 + gi
            ch0 = (mt % 4) * 32 + gi * 16
            for ri, (bsrc, csrc) in enumerate(((p["s5_b_re"], p["s5_c_re"]), (p["s5_b_im"], p["s5_c_im"]))):
                sB[ch0:ch0 + 16, mt, ri, gi * 64:(gi + 1) * 64] = bsrc[l][g].T
                sC[gi * 64:(gi + 1) * 64, mt, ri, ch0:ch0 + 16] = csrc[l][g].T
            par[gi * 64:(gi + 1) * 64, 0, mt] = p["s5_a_re"][l][g]
            par[gi * 64:(gi + 1) * 64, 1, mt] = p["s5_a_im"][l][g]
            par[gi * 64:(gi + 1) * 64, 2, mt] = p["s5_log_step"][l][g]
    m["s5_B"], m["s5_C"], m["s5_par"] = sB, sC, par
    mv[:, MV["s5"]:MV["s5"] + 2] = vec_fm(p["s5_d"][l][hv])
    o = IN_OFF[10]
    m["rw_w"] = np.concatenate([win[:, o:o + 512][:, hv], win[:, o + 512:o + 1024][:, hv], win[:, o + 1024:o + 1536][:, hv],
                                win[:, o + 1536:o + 1664], win[:, o + 1664:o + 1792]], axis=1)
    mu = p["rwkv_mu"][l]
    mucat = np.concatenate([mu[0:512][hv], mu[512:1024][hv], mu[1024:1536][hv], mu[1536:1664], mu[1664:1792]])
    vb = MV["rw"]
    mv[:, vb:vb + 8] = vec_fm(mucat)
    m["rw_wa"] = np.concatenate([p["rwkv_w_up"][l][:, hv], p["rwkv_a_up"][l][:, hv]], axis=0)
    m["rw_gup"] = p["rwkv_g_up"][l][:, hv]
    for i, nm in enumerate(("rwkv_w0", "rwkv_a0", "rwkv_k_k", "rwkv_k_a")):
        mv[:, vb + 8 + 2 * i:vb + 10 + 2 * i] = vec_fm(p[nm][l][hv])
    mv[:, vb + 16:vb + 18] = vec_fm(p["rwkv_r_k"][l].reshape(-1)[hv])
    mv[:, vb + 18:vb + 20] = vec_fm(p["rwkv_norm"][l][hv])
    m["mvecs"] = mv
    return {k: (v if k == "hT" else np.ascontiguousarray(v, dtype=np.float32)) for k, v in m.items()}


def gla_setup(mx):
    S = mx.S
    st = {}
    st["w"] = S.sb("glw", [128, 8, 896], BF16)
    S.dma(st["w"].t[:], wview(mx.din("gla_w", [1024, 896]), 0, 896), writes=[st["w"]], q="pool")
    st["aup"] = S.sb("glaup", [128, 128], BF16)
    S.dma(st["aup"].t[:], mx.din("gla_aup", [128, 128]), writes=[st["aup"]], q="pool")
    st["S"] = S.sb("glS", [128, 128], F32)
    st["Sb"] = S.sb("glSb", [128, 128], BF16)
    S.o("dve", "memset", writes=[st["S"]], ap=st["S"].t[:], constant=0.0)
    S.o("dve", "memset", writes=[st["Sb"]], ap=st["Sb"].t[:], constant=0.0)
    st["Vg"] = S.sb("glVg", [64, 8, 256], BF16)
    st["ktm"] = S.sb("glktm", [64, 128], BF16)
    st["Wt"] = S.sb("glWt", [64, 128], BF16)
    st["cab"] = S.sb("glcab", [128, SEG], BF16)
    st["kth"] = [S.sb("glkth", [128, SEG], BF16) for _ in range(2)]
    st["Sbh"] = [S.sb("glSbh", [128, 128], BF16) for _ in range(2)]
    for b in st["kth"] + st["Sbh"]:
        S.o("dve", "memset", writes=[b], ap=b.t[:], constant=0.0)
    st["nb"] = S.sb("glnb", [128, 1], F32)
    S.o("dve", "tensor_scalar", [mx.vec], [st["nb"]], out=st["nb"].t[:], in0=mx.v(MV["gla"]), scalar1=-1.0,
        scalar2=None, op0=ALU.mult)
    return st


def gla_seg(mx, st, hs, sg):
    S = mx.S
    P, F, B = mx.P, mx.F, mx.B
    w = st["w"]
    m64 = mx.c["uincl"]
    Sf, Sb, Vg = st["S"], st["Sb"], st["Vg"]
    mx.proj(w, 768, 128, hs, P[0])
    S.o("act", "activation", [P[0]], [st["cab"]], out=st["cab"].t[:], in_=P[0].t[:, :], func=AF.Copy)
    S.mm(P[1].t[:], st["aup"].t[:], st["cab"].t[:], True, True, [st["aup"], st["cab"]], [P[1]])
    e1, b, eb, enb = F[0], F[1], F[2], F[3]
    S.o("act", "activation", [P[1], st["nb"]], [e1], out=e1.t[:], in_=P[1].t[:], func=AF.Exp, scale=-1.0,
        bias=st["nb"].t[:, 0:1])
    S.o("act", "activation", [e1, mx.one1], [e1], out=e1.t[:], in_=e1.t[:], func=AF.Ln, bias=mx.one1.t[:, 0:1])
    S.o("dve", "tensor_scalar", [e1], [e1], out=e1.t[:], in0=e1.t[:], scalar1=-1.0 / 16.0, scalar2=None, op0=ALU.mult)
    S.o("dve", "tensor_tensor_scan", [mx.c["rm64"], e1], [b], out=b.t[:], data0=mx.c["rm64"].t[:], data1=e1.t[:],
        initial=0.0, op0=ALU.mult, op1=ALU.add)
    S.o("act", "activation", [b], [eb], out=eb.t[:], in_=b.t[:], func=AF.Exp)
    S.o("act", "activation", [b], [enb], out=enb.t[:], in_=b.t[:], func=AF.Exp, scale=-1.0)
    GD = int(os.environ.get('GD', '99'))
    if GD <= 1:
        return
    mx.proj(w, 0, 128, hs, P[2])
    mx.proj(w, 128, 128, hs, P[3])
    qt, kt = B[0], B[1]
    S.o("dve", "scalar_tensor_tensor", [P[2], eb], [qt], out=qt.t[:], in0=P[2].t[:], scalar=0.125, in1=eb.t[:],
        op0=ALU.mult, op1=ALU.mult)
    S.o("dve", "tensor_tensor", [P[3], enb], [kt], out=kt.t[:], in0=P[3].t[:], in1=enb.t[:], op=ALU.mult)
    kth = st["kth"]
    for hd in range(2):
        hp = slice(hd * 64, (hd + 1) * 64)
        S.o("dve", "tensor_copy", [kt], [kth[hd]], out=kth[hd].t[hp, :], in_=kt.t[hp, :])
    if GD <= 2:
        return
    for c2 in range(4):
        pv = P[4 + c2 % 2]
        for cc in range(2):
            c = c2 * 2 + cc
            for kc in range(8):
                S.mm(pv.t[0:64, cc * 256:(cc + 1) * 256], hs.t[:, kc, c * 64:(c + 1) * 64], w.t[:, kc, 512:768],
                     kc == 0, kc == 7, [hs, w], [pv])
        S.o("act", "activation", [pv], [Vg], out=Vg.t[:, c2 * 2:c2 * 2 + 2, :],
            in_=pv.t[0:64, :].rearrange("p (c v) -> p c v", c=2), func=AF.Copy)
    if GD <= 3:
        return
    PSC, PS_, PO = P[0], P[1], [P[2], P[3]]
    for c in range(8):
        cs = slice(c * 64, (c + 1) * 64)
        S.o("pe", "transpose", [kt, mx.identb], [mx.PB], out=mx.PB.t[0:64, 0:128], in_=kt.t[:, cs], identity=mx.identb.t[:])
        S.o("act", "activation", [mx.PB], [st["ktm"]], out=st["ktm"].t[:], in_=mx.PB.t[0:64, 0:128], func=AF.Copy)
        if GD <= 4:
            continue
        for hd in range(2):
            hp = slice(hd * 64, (hd + 1) * 64)
            S.mm(PSC.t[0:64, hd * 64:(hd + 1) * 64], kth[hd].t[:, cs], qt.t[:, cs], True, True, [kth[hd], qt], [PSC])
        for hd in range(2):
            S.o("dve", "tensor_tensor", [PSC, m64], [st["Wt"]], out=st["Wt"].t[:, hd * 64:(hd + 1) * 64],
                in0=PSC.t[0:64, hd * 64:(hd + 1) * 64], in1=m64.t[0:64, 0:64], op=ALU.mult)
        if GD <= 5:
            continue
        for hd in range(2):
            hp = slice(hd * 64, (hd + 1) * 64)
            S.mm(PO[hd].t[:, cs], Vg.t[:, c, hd * 128:(hd + 1) * 128], st["Wt"].t[:, hd * 64:(hd + 1) * 64], True, False,
                 [Vg, st["Wt"]], [PO[hd]])
            S.mm(PO[hd].t[:, cs], st["Sbh"][hd].t[:], qt.t[:, cs], False, True, [st["Sbh"][hd], qt], [PO[hd]])
        if GD <= 6:
            continue
        S.mm(PS_.t[:, 0:256], st["ktm"].t[:], Vg.t[:, c, :], True, True, [st["ktm"], Vg], [PS_])
        for hd in range(2):
            hp = slice(hd * 64, (hd + 1) * 64)
            ebl = eb.t[hp, c * 64 + 63:c * 64 + 64]
            S.o("dve", "tensor_scalar", [Sf, eb], [Sf], out=Sf.t[hp, :], in0=Sf.t[hp, :], scalar1=ebl, scalar2=None,
                op0=ALU.mult)
            S.o("dve", "scalar_tensor_tensor", [PS_, eb, Sf], [Sf], out=Sf.t[hp, :], in0=PS_.t[hp, hd * 128:(hd + 1) * 128],
                scalar=ebl, in1=Sf.t[hp, :], op0=ALU.mult, op1=ALU.add)
        for hd in range(2):
            hp = slice(hd * 64, (hd + 1) * 64)
            S.o("act", "activation", [Sf], [st["Sbh"][hd]], out=st["Sbh"][hd].t[hp, :], in_=Sf.t[hp, :], func=AF.Copy)
    if GD <= 7:
        return
    for hd in range(2):
        mx.proj(w, 256 + hd * 128, 128, hs, P[4])
        gg = F[4]
        S.o("act", "activation", [P[4]], [gg], out=gg.t[:], in_=P[4].t[:], func=AF.Silu)
        oo = F[5]
        S.o("act", "activation", [PO[hd]], [oo], out=oo.t[:], in_=PO[hd].t[:], func=AF.Copy)
        ob = B[2 + hd]
        mx.headnorm(oo, 128, mx.v(MV["gla"] + 1 + hd), gg, ob, P[5], B[4], F[6])
        S.dma(mx.hid_dst(512 + hd * 128, sg), ob.t[:], reads=[ob])


TWO_PI = 2.0 * np.pi


def s5_setup(mx):
    S = mx.S
    st = {}
    st["w"] = S.sb("s5w", [128, 8, 256], BF16)
    S.dma(st["w"].t[:], wview(mx.din("s5_w", [1024, 256]), 0, 256), writes=[st["w"]], q="pool")
    st["BT"] = S.sb("s5BT", [128, 8, 2, 128], BF16)
    S.dma(st["BT"].t[:], mx.din("s5_B", [128, 8, 2, 128]), writes=[st["BT"]], q="pool")
    Cf = S.sb("s5Cf", [128, 8, 2, 128], F32)
    S.dma(Cf.t[:], mx.din("s5_C", [128, 8, 2, 128]), writes=[Cf])
    par = S.sb("s5par", [128, 3, 8], F32)
    S.dma(par.t[:], mx.din("s5_par", [128, 3, 8]), writes=[par])
    sm = [S.sb("s5sm", [128, 8], F32) for _ in range(14)]
    smi = S.sb("s5smi", [128, 8], I32)

    def ts(out, in0, s1, op0, s2=None, op1=None):
        kw = dict(out=out.t[:], in0=in0.t[:] if isinstance(in0, Buf) else in0, scalar1=s1, scalar2=s2, op0=op0)
        if op1 is not None:
            kw["op1"] = op1
        S.o("dve", "tensor_scalar", [in0] if isinstance(in0, Buf) else [par], [out], **kw)

    def tt(out, a, b, op):
        S.o("dve", "tensor_tensor", [a, b], [out], out=out.t[:], in0=a.t[:], in1=b.t[:], op=op)

    lre, step, r, th, cs_, sn, t0, t1, den, cre, cim, bre, bim, t2 = sm
    ts(lre, par.t[:, 0, :], -1e-4, ALU.min)
    S.o("act", "activation", [par], [step], out=step.t[:], in_=par.t[:, 2, :], func=AF.Exp)
    tt(t0, lre, step, ALU.mult)
    S.o("act", "activation", [t0], [r], out=r.t[:], in_=t0.t[:], func=AF.Exp)
    S.o("dve", "tensor_tensor", [par, step], [th], out=th.t[:], in0=par.t[:, 1, :], in1=step.t[:], op=ALU.mult)

    def sin_of(out, ang):
        ts(t0, ang, 1.0 / TWO_PI, ALU.mult)
        S.o("dve", "tensor_copy", [t0], [smi], out=smi.t[:], in_=t0.t[:])
        S.o("dve", "tensor_copy", [smi], [t0], out=t0.t[:], in_=smi.t[:])
        S.o("dve", "scalar_tensor_tensor", [t0, ang], [t1], out=t1.t[:], in0=t0.t[:], scalar=-TWO_PI, in1=ang.t[:],
            op0=ALU.mult, op1=ALU.add)
        ts(t0, t1, float(np.pi), ALU.is_gt, -TWO_PI, ALU.mult)
        tt(t1, t1, t0, ALU.add)
        ts(t0, t1, float(-np.pi), ALU.is_lt, TWO_PI, ALU.mult)
        tt(t1, t1, t0, ALU.add)
        S.o("act", "activation", [t1], [out], out=out.t[:], in_=t1.t[:], func=AF.Sin)

    sin_of(sn, th)
    ts(t2, th, float(np.pi / 2), ALU.add)
    sin_of(cs_, t2)
    tt(bre, r, cs_, ALU.mult)
    tt(bim, r, sn, ALU.mult)
    lim = Buf(par.t[:, 1, :])
    lim.w = par.w
    tt(t0, lre, lre, ALU.mult)
    S.o("dve", "tensor_tensor", [par], [t1], out=t1.t[:], in0=par.t[:, 1, :], in1=par.t[:, 1, :], op=ALU.mult)
    tt(den, t0, t1, ALU.add)
    S.o("dve", "reciprocal", [den], [den], out=den.t[:], in_=den.t[:])
    ts(t2, bre, -1.0, ALU.add)
    tt(t0, t2, lre, ALU.mult)
    S.o("dve", "tensor_tensor", [bim, par], [t1], out=t1.t[:], in0=bim.t[:], in1=par.t[:, 1, :], op=ALU.mult)
    tt(cre, t0, t1, ALU.add)
    tt(cre, cre, den, ALU.mult)
    tt(t0, bim, lre, ALU.mult)
    S.o("dve", "tensor_tensor", [t2, par], [t1], out=t1.t[:], in0=t2.t[:], in1=par.t[:, 1, :], op=ALU.mult)
    tt(cim, t0, t1, ALU.subtract)
    tt(cim, cim, den, ALU.mult)
    Cp = S.sb("s5Cp", [128, 8, 2, 128], BF16)
    X1 = S.sb("s5X1", [128, 8, 128], F32)
    X2 = S.sb("s5X2", [128, 8, 128], F32)
    bc = lambda t: t.t[:].unsqueeze(2).to_broadcast([128, 8, 128])
    S.o("dve", "tensor_tensor", [Cf, cre], [X1], out=X1.t[:], in0=Cf.t[:, :, 0, :], in1=bc(cre), op=ALU.mult)
    S.o("dve", "tensor_tensor", [Cf, cim], [X2], out=X2.t[:], in0=Cf.t[:, :, 1, :], in1=bc(cim), op=ALU.mult)
    S.o("dve", "tensor_tensor", [X1, X2], [Cp], out=Cp.t[:, :, 0, :], in0=X1.t[:], in1=X2.t[:], op=ALU.subtract)
    S.o("dve", "tensor_tensor", [Cf, cim], [X1], out=X1.t[:], in0=Cf.t[:, :, 0, :], in1=bc(cim), op=ALU.mult)
    S.o("dve", "tensor_tensor", [Cf, cre], [X2], out=X2.t[:], in0=Cf.t[:, :, 1, :], in1=bc(cre), op=ALU.mult)
    S.o("dve", "tensor_tensor", [X1, X2], [X1], out=X1.t[:], in0=X1.t[:], in1=X2.t[:], op=ALU.add)
    S.o("dve", "tensor_scalar", [X1], [Cp], out=Cp.t[:, :, 1, :], in0=X1.t[:], scalar1=-1.0, scalar2=None, op0=ALU.mult)
    Er = S.sb("s5Er", [128, 8, SEG], F32)
    Ei = S.sb("s5Ei", [128, 8, SEG], F32)
    S.o("dve", "memset", writes=[Er], ap=Er.t[:, :, 0:1], constant=1.0)
    S.o("dve", "memset", writes=[Ei], ap=Ei.t[:, :, 0:1], constant=0.0)
    ck, sk = S.sb("s5ck", [128, 8], F32), S.sb("s5sk", [128, 8], F32)
    S.o("dve", "tensor_copy", [cs_], [ck], out=ck.t[:], in_=cs_.t[:])
    S.o("dve", "tensor_copy", [sn], [sk], out=sk.t[:], in_=sn.t[:])
    Y1 = S.sb("s5Y1", [128, 8, SEG // 2], F32)
    Y2 = S.sb("s5Y2", [128, 8, SEG // 2], F32)
    d_ = 1
    while d_ < SEG:
        bcd = lambda t: t.t[:].unsqueeze(2).to_broadcast([128, 8, d_])
        S.o("dve", "tensor_tensor", [Er, ck], [Y1], out=Y1.t[:, :, 0:d_], in0=Er.t[:, :, 0:d_], in1=bcd(ck), op=ALU.mult)
        S.o("dve", "tensor_tensor", [Ei, sk], [Y2], out=Y2.t[:, :, 0:d_], in0=Ei.t[:, :, 0:d_], in1=bcd(sk), op=ALU.mult)
        S.o("dve", "tensor_tensor", [Y1, Y2], [Er], out=Er.t[:, :, d_:2 * d_], in0=Y1.t[:, :, 0:d_], in1=Y2.t[:, :, 0:d_],
            op=ALU.subtract)
        S.o("dve", "tensor_tensor", [Er, sk], [Y1], out=Y1.t[:, :, 0:d_], in0=Er.t[:, :, 0:d_], in1=bcd(sk), op=ALU.mult)
        S.o("dve", "tensor_tensor", [Ei, ck], [Y2], out=Y2.t[:, :, 0:d_], in0=Ei.t[:, :, 0:d_], in1=bcd(ck), op=ALU.mult)
        S.o("dve", "tensor_tensor", [Y1, Y2], [Ei], out=Ei.t[:, :, d_:2 * d_], in0=Y1.t[:, :, 0:d_], in1=Y2.t[:, :, 0:d_],
            op=ALU.add)
        tt(t0, ck, ck, ALU.mult)
        tt(t1, sk, sk, ALU.mult)
        tt(t2, ck, sk, ALU.mult)
        tt(ck, t0, t1, ALU.subtract)
        ts(sk, t2, 2.0, ALU.mult)
        d_ *= 2
    st.update(Cp=Cp, Er=Er, Ei=Ei, r=r, e5r=ck, e5i=sk)
    st["car"] = S.sb("s5car", [128, 8], F32)
    st["cai"] = S.sb("s5cai", [128, 8], F32)
    S.o("dve", "memset", writes=[st["car"]], ap=st["car"].t[:], constant=0.0)
    S.o("dve", "memset", writes=[st["cai"]], ap=st["cai"].t[:], constant=0.0)
    st["xsb"] = S.sb("s5xsb", [128, 2, SEG], BF16)
    st["xf"] = S.sb("s5xf", [128, 2, SEG], F32)
    st["tmp"] = S.sb("s5tmp", [128, 4], F32)
    st["tmp2"] = [S.sb("s5tmp2", [128, 4], F32) for _ in range(2)]
    st["X"] = [S.sb("s5X", [128, SEG], F32) for _ in range(8)]
    return st


def s5_seg(mx, st, hs, sg):
    S = mx.S
    P, F, B = mx.P, mx.F, mx.B
    w, BT, Cp, Er, Ei, r = st["w"], st["BT"], st["Cp"], st["Er"], st["Ei"], st["r"]
    xsb, xf, car, cai, tmp = st["xsb"], st["xf"], st["car"], st["cai"], st["tmp"]
    for kc in range(2):
        mx.proj(w, kc * 128, 128, hs, P[0])
        S.o("act", "activation", [P[0]], [xsb], out=xsb.t[:, kc, :], in_=P[0].t[:], func=AF.Copy)
        S.o("dve", "tensor_copy", [P[0]], [xf], out=xf.t[:, kc, :], in_=P[0].t[:])
    PY = [P[5], P[6]]
    L = SEG - 1
    X = st["X"]
    for mp in range(4):
        ms = (2 * mp, 2 * mp + 1)
        kc = mp // 2
        bufs = {}
        for i, m in enumerate(ms):
            fb = (F[0:8] if i == 0 else X[0:8])
            bufs[m] = dict(t1=fb[0], t2=fb[1], t3=fb[2], t4=fb[3], ur=fb[4], ui=fb[5], wr=fb[6], wi=fb[7],
                           pr=P[1 + i * 2], pi=P[2 + i * 2], sr=B[0 + i * 2], si=B[1 + i * 2])
        for m in ms:
            b = bufs[m]
            S.mm(b["pr"].t[:], BT.t[:, m, 0, :], xsb.t[:, kc, :], True, True, [BT, xsb], [b["pr"]])
            S.mm(b["pi"].t[:], BT.t[:, m, 1, :], xsb.t[:, kc, :], True, True, [BT, xsb], [b["pi"]])
        for m in ms:
            b = bufs[m]
            er, ei = Er.t[:, m, :], Ei.t[:, m, :]
            S.o("dve", "tensor_tensor", [b["pr"], Er], [b["t1"]], out=b["t1"].t[:], in0=b["pr"].t[:], in1=er, op=ALU.mult)
            S.o("dve", "tensor_tensor", [b["pi"], Ei], [b["t2"]], out=b["t2"].t[:], in0=b["pi"].t[:], in1=ei, op=ALU.mult)
            S.o("dve", "tensor_tensor", [b["pi"], Er], [b["t3"]], out=b["t3"].t[:], in0=b["pi"].t[:], in1=er, op=ALU.mult)
            S.o("dve", "tensor_tensor", [b["pr"], Ei], [b["t4"]], out=b["t4"].t[:], in0=b["pr"].t[:], in1=ei, op=ALU.mult)
        for m in ms:
            b = bufs[m]
            S.o("pool", "tensor_tensor", [b["t1"], b["t2"]], [b["ur"]], out=b["ur"].t[:], in0=b["t1"].t[:], in1=b["t2"].t[:],
                op=ALU.add)
            S.o("pool", "tensor_tensor", [b["t3"], b["t4"]], [b["ui"]], out=b["ui"].t[:], in0=b["t3"].t[:], in1=b["t4"].t[:],
                op=ALU.subtract)
        for m in ms:
            b = bufs[m]
            wr, wi = b["wr"], b["wi"]
            rb = r.t[:, m:m + 1].to_broadcast([128, SEG])
            S.o("dve", "tensor_tensor_scan", [r, b["ur"], car], [wr], out=wr.t[:], data0=rb, data1=b["ur"].t[:],
                initial=car.t[:, m:m + 1], op0=ALU.mult, op1=ALU.add)
            S.o("dve", "tensor_tensor_scan", [r, b["ui"], cai], [wi], out=wi.t[:], data0=rb, data1=b["ui"].t[:],
                initial=cai.t[:, m:m + 1], op0=ALU.mult, op1=ALU.add)
            e5r, e5i = st["e5r"].t[:, m:m + 1], st["e5i"].t[:, m:m + 1]
            tm = st["tmp2"][m % 2]
            S.o("dve", "tensor_tensor", [wr, st["e5r"]], [tm], out=tm.t[:, 0:1], in0=wr.t[:, L:L + 1], in1=e5r, op=ALU.mult)
            S.o("dve", "tensor_tensor", [wi, st["e5i"]], [tm], out=tm.t[:, 1:2], in0=wi.t[:, L:L + 1], in1=e5i, op=ALU.mult)
            S.o("dve", "tensor_tensor", [wr, st["e5i"]], [tm], out=tm.t[:, 2:3], in0=wr.t[:, L:L + 1], in1=e5i, op=ALU.mult)
            S.o("dve", "tensor_tensor", [wi, st["e5r"]], [tm], out=tm.t[:, 3:4], in0=wi.t[:, L:L + 1], in1=e5r, op=ALU.mult)
            S.o("dve", "tensor_tensor", [tm], [car], out=car.t[:, m:m + 1], in0=tm.t[:, 0:1], in1=tm.t[:, 1:2],
                op=ALU.subtract)
            S.o("dve", "tensor_tensor", [tm], [cai], out=cai.t[:, m:m + 1], in0=tm.t[:, 2:3], in1=tm.t[:, 3:4], op=ALU.add)
        for m in ms:
            b = bufs[m]
            er, ei = Er.t[:, m, :], Ei.t[:, m, :]
            wr, wi = b["wr"], b["wi"]
            S.o("dve", "tensor_tensor", [wr, Er], [b["t1"]], out=b["t1"].t[:], in0=wr.t[:], in1=er, op=ALU.mult)
            S.o("dve", "tensor_tensor", [wi, Ei], [b["t2"]], out=b["t2"].t[:], in0=wi.t[:], in1=ei, op=ALU.mult)
            S.o("dve", "tensor_tensor", [wr, Ei], [b["t3"]], out=b["t3"].t[:], in0=wr.t[:], in1=ei, op=ALU.mult)
            S.o("dve", "tensor_tensor", [wi, Er], [b["t4"]], out=b["t4"].t[:], in0=wi.t[:], in1=er, op=ALU.mult)
        for m in ms:
            b = bufs[m]
            S.o("pool", "tensor_tensor", [b["t1"], b["t2"]], [b["sr"]], out=b["sr"].t[:], in0=b["t1"].t[:], in1=b["t2"].t[:],
                op=ALU.subtract)
            S.o("pool", "tensor_tensor", [b["t3"], b["t4"]], [b["si"]], out=b["si"].t[:], in0=b["t3"].t[:], in1=b["t4"].t[:],
                op=ALU.add)
        for m in ms:
            b = bufs[m]
            S.mm(PY[kc].t[:], Cp.t[:, m, 0, :], b["sr"].t[:], m % 4 == 0, False, [Cp, b["sr"]], [PY[kc]])
            S.mm(PY[kc].t[:], Cp.t[:, m, 1, :], b["si"].t[:], False, m % 4 == 3, [Cp, b["si"]], [PY[kc]])
    for kc in range(2):
        y, y2 = F[8], F[9]
        S.o("dve", "scalar_tensor_tensor", [xf, mx.vec, PY[kc]], [y], out=y.t[:], in0=xf.t[:, kc, :],
            scalar=mx.v(MV["s5"] + kc), in1=PY[kc].t[:], op0=ALU.mult, op1=ALU.add)
        S.o("act", "activation", [y], [y2], out=y2.t[:], in_=y.t[:], func=AF.Square)
        S.o("dve", "tensor_scalar", [y2], [y2], out=y2.t[:], in0=y2.t[:], scalar1=0.044715, scalar2=1.0, op0=ALU.mult,
            op1=ALU.add)
        S.o("dve", "tensor_tensor", [y2, y], [y2], out=y2.t[:], in0=y2.t[:], in1=y.t[:], op=ALU.mult)
        S.o("act", "activation", [y2], [y2], out=y2.t[:], in_=y2.t[:], func=AF.Sigmoid, scale=2.0 * (2.0 / np.pi) ** 0.5)
        zb = B[4 + kc]
        S.o("dve", "tensor_tensor", [y2, y], [zb], out=zb.t[:], in0=y2.t[:], in1=y.t[:], op=ALU.mult)
        S.dma(mx.hid_dst(256 + kc * 128, sg), zb.t[:], reads=[zb])


NEG_EXP_HALF = -float(np.exp(-0.5))


def rwkv_setup(mx):
    S = mx.S
    st = {}
    st["w"] = S.sb("rww", [128, 8, 1024], BF16)
    S.dma(st["w"].t[:], wview(mx.din("rw_w", [1024, 1024]), 0, 1024), writes=[st["w"]], q="pool")
    st["wa"] = S.sb("rwwa", [128, 256], BF16)
    S.dma(st["wa"].t[:], mx.din("rw_wa", [128, 256]), writes=[st["wa"]], q="pool")
    st["gup"] = S.sb("rwgup", [128, 256], BF16)
    S.dma(st["gup"].t[:], mx.din("rw_gup", [128, 256]), writes=[st["gup"]], q="pool")
    st["zp"] = S.sb("rwzp", [128, SEG + 1], F32)
    st["carry"] = S.sb("rwcarry", [128, 8], F32)
    S.o("dve", "memset", writes=[st["carry"]], ap=st["carry"].t[:], constant=0.0)
    st["R"] = [S.sb("rwR", [128, SEG], F32) for _ in range(6)]
    st["WAb"] = S.sb("rwWAb", [128, SEG], BF16)
    st["XGb"] = S.sb("rwXGb", [128, SEG], BF16)
    mk = lambda n: S.sb(n, [128, 128], F32)
    mkb = lambda n: S.sb(n, [128, 128], BF16)
    st["ST"] = [mk("rwST") for _ in range(2)]
    st["Vw"] = [[mkb("rwVw") for _ in range(2)] for _ in range(2)]
    st["Uw"] = [mkb("rwUw") for _ in range(2)]
    st["STb"] = [mkb("rwSTb") for _ in range(2)]
    st["Rb"] = [S.sb("rwRb", [128, SEG], BF16) for _ in range(5)]
    for b in st["ST"] + st["Vw"][0] + st["Vw"][1] + st["Uw"] + st["STb"]:
        S.o("dve", "memset", writes=[b], ap=b.t[:], constant=0.0)
    for n in ("btm", "ktm", "vtm"):
        st[n] = [mkb("rw" + n) for _ in range(2)]
    st["Ua"] = mkb("rwUa")
    st["Xs"] = mkb("rwXs")
    st["AkT"] = [mkb("rwAkT") for _ in range(4)]
    st["BB"] = [S.sb("rwBB", [128, 2, 128], BF16) for _ in range(4)]
    st["MM"] = [[S.sb("rwMM", [128, 2, 128], BF16) for _ in range(2)] for _ in range(4)]
    st["QQ"] = [[S.sb("rwQQ", [128, 2, 128], BF16) for _ in range(2)] for _ in range(4)]
    st["mask_lu"] = S.sb("rwmlu", [128, 2, 128], F32)
    st["mask_uu"] = S.sb("rwmuu", [128, 2, 128], F32)
    st["id2"] = S.sb("rwid2", [128, 2, 128], BF16)
    S.o("dve", "tensor_copy", [mx.c["lstrict"]], [st["mask_lu"]], out=st["mask_lu"].t[:, 0, :], in_=mx.c["lstrict"].t[:])
    S.o("dve", "tensor_copy", [mx.c["ustrict"]], [st["mask_lu"]], out=st["mask_lu"].t[:, 1, :], in_=mx.c["ustrict"].t[:])
    for i in range(2):
        S.o("dve", "tensor_copy", [mx.c["uincl"]], [st["mask_uu"]], out=st["mask_uu"].t[:, i, :], in_=mx.c["uincl"].t[:])
        S.o("dve", "tensor_copy", [mx.identb], [st["id2"]], out=st["id2"].t[:, i, :], in_=mx.identb.t[:])
    st["gneps"] = S.sb("rwgneps", [128, 1], F32)
    S.o("dve", "memset", writes=[st["gneps"]], ap=st["gneps"].t[:], constant=64e-5)
    return st


def rwkv_seg(mx, st, hs, sg):
    S = mx.S
    P, F, B = mx.P, mx.F, mx.B
    R = st["R"]
    w, zp, carry = st["w"], st["zp"], st["carry"]
    vb = MV["rw"]
    idf, bonesf = mx.identf, mx.c["bones"]
    lstrict, ustrict, uincl = mx.c["lstrict"], mx.c["ustrict"], mx.c["uincl"]
    pk = [0]

    def nextP():
        pk[0] = (pk[0] + 1) % 2
        return P[4 + pk[0]]

    def shifted(g, out):
        pz = nextP()
        mx.proj(w, g * 128, 128, hs, pz)
        S.o("act", "activation", [pz], [zp], out=zp.t[:, 1:SEG + 1], in_=pz.t[:], func=AF.Copy)
        S.o("dve", "tensor_copy", [carry], [zp], out=zp.t[:, 0:1], in_=carry.t[:, g:g + 1])
        S.o("dve", "tensor_copy", [zp], [carry], out=carry.t[:, g:g + 1], in_=zp.t[:, SEG:SEG + 1])
        S.o("dve", "tensor_tensor", [zp], [out], out=out.t[:], in0=zp.t[:, 0:SEG], in1=zp.t[:, 1:SEG + 1], op=ALU.subtract)
        S.o("dve", "scalar_tensor_tensor", [out, mx.vec, zp], [out], out=out.t[:], in0=out.t[:], scalar=mx.v(vb + g),
            in1=zp.t[:, 1:SEG + 1], op0=ALU.mult, op1=ALU.add)

    WAb, XGb = st["WAb"], st["XGb"]
    shifted(6, R[5])
    S.o("act", "activation", [R[5]], [WAb], out=WAb.t[0:64, :], in_=R[5].t[0:64, :], func=AF.Tanh)
    S.o("act", "activation", [R[5]], [WAb], out=WAb.t[64:128, :], in_=R[5].t[64:128, :], func=AF.Copy)
    shifted(7, R[5])
    S.o("act", "activation", [R[5]], [XGb], out=XGb.t[:], in_=R[5].t[:], func=AF.Sigmoid)
    for p2 in range(2):
        cols = slice(p2 * 128, (p2 + 1) * 128)
        r_, k_, v_, a_, g_, kk, krep, ld, cum, tmp = F
        _al, _be, _ka, _rt, bonus, tmp2 = R
        al, be, ka, rt, vbf = st["Rb"]
        shifted(0 + p2, r_)
        shifted(2 + p2, k_)
        shifted(4 + p2, v_)
        pz = nextP()
        S.mm(pz.t[:], st["wa"].t[0:64, cols], WAb.t[0:64, :], True, True, [st["wa"], WAb], [pz])
        S.o("act", "activation", [pz, mx.vec], [ld], out=ld.t[:], in_=pz.t[:], func=AF.Sigmoid, bias=mx.v(vb + 8 + p2))
        S.o("dve", "tensor_scalar", [ld], [ld], out=ld.t[:], in0=ld.t[:], scalar1=NEG_EXP_HALF, scalar2=None, op0=ALU.mult)
        pz = nextP()
        S.mm(pz.t[:], st["wa"].t[64:128, cols], WAb.t[64:128, :], True, True, [st["wa"], WAb], [pz])
        S.o("act", "activation", [pz, mx.vec], [a_], out=a_.t[:], in_=pz.t[:], func=AF.Sigmoid, bias=mx.v(vb + 10 + p2))
        pz = nextP()
        S.mm(pz.t[:], st["gup"].t[:, cols], XGb.t[:], True, True, [st["gup"], XGb], [pz])
        S.o("act", "activation", [pz], [g_], out=g_.t[:], in_=pz.t[:], func=AF.Copy)
        S.o("dve", "tensor_scalar", [k_, mx.vec], [kk], out=kk.t[:], in0=k_.t[:], scalar1=mx.v(vb + 12 + p2), scalar2=None,
            op0=ALU.mult)
        S.o("act", "activation", [kk], [tmp], out=tmp.t[:], in_=kk.t[:], func=AF.Square)
        pz = nextP()
        S.mm(pz.t[:], bonesf.t[:], tmp.t[:], True, True, [bonesf, tmp], [pz])
        S.o("act", "activation", [pz], [tmp], out=tmp.t[:], in_=pz.t[:], func=AF.Sqrt)
        S.o("dve", "tensor_scalar", [tmp], [tmp], out=tmp.t[:], in0=tmp.t[:], scalar1=1e-12, scalar2=None, op0=ALU.max)
        S.o("dve", "reciprocal", [tmp], [tmp], out=tmp.t[:], in_=tmp.t[:])
        S.o("dve", "tensor_tensor", [kk, tmp], [kk], out=kk.t[:], in0=kk.t[:], in1=tmp.t[:], op=ALU.mult)
        S.o("dve", "tensor_scalar", [a_, mx.vec], [krep], out=krep.t[:], in0=a_.t[:], scalar1=-1.0,
            scalar2=mx.v(vb + 14 + p2), op0=ALU.add, op1=ALU.mult)
        S.o("dve", "scalar_tensor_tensor", [krep, k_], [krep], out=krep.t[:], in0=krep.t[:], scalar=1.0, in1=k_.t[:],
            op0=ALU.add, op1=ALU.mult)
        S.o("dve", "tensor_tensor_scan", [mx.c["rm128"], ld], [cum], out=cum.t[:], data0=mx.c["rm128"].t[:], data1=ld.t[:],
            initial=0.0, op0=ALU.mult, op1=ALU.add)
        S.o("dve", "tensor_tensor", [cum, ld], [tmp], out=tmp.t[:], in0=cum.t[:], in1=ld.t[:], op=ALU.subtract)
        S.o("act", "activation", [tmp], [tmp], out=tmp.t[:], in_=tmp.t[:], func=AF.Exp)
        S.o("dve", "scalar_tensor_tensor", [kk, tmp], [al], out=al.t[:], in0=kk.t[:], scalar=-1.0, in1=tmp.t[:],
            op0=ALU.mult, op1=ALU.mult)
        S.o("act", "activation", [cum], [tmp], out=tmp.t[:], in_=cum.t[:], func=AF.Exp, scale=-1.0)
        S.o("dve", "tensor_tensor", [kk, a_], [_be], out=_be.t[:], in0=kk.t[:], in1=a_.t[:], op=ALU.mult)
        S.o("dve", "tensor_tensor", [_be, tmp], [be], out=be.t[:], in0=_be.t[:], in1=tmp.t[:], op=ALU.mult)
        S.o("dve", "tensor_tensor", [krep, tmp], [ka], out=ka.t[:], in0=krep.t[:], in1=tmp.t[:], op=ALU.mult)
        S.o("act", "activation", [cum], [tmp2], out=tmp2.t[:], in_=cum.t[:], func=AF.Exp)
        S.o("dve", "tensor_tensor", [r_, tmp2], [rt], out=rt.t[:], in0=r_.t[:], in1=tmp2.t[:], op=ALU.mult)
        S.o("dve", "scalar_tensor_tensor", [r_, mx.vec, krep], [tmp], out=tmp.t[:], in0=r_.t[:], scalar=mx.v(vb + 16 + p2),
            in1=krep.t[:], op0=ALU.mult, op1=ALU.mult)
        pz = nextP()
        S.mm(pz.t[:], bonesf.t[:], tmp.t[:], True, True, [bonesf, tmp], [pz])
        S.o("dve", "tensor_tensor", [pz, v_], [bonus], out=bonus.t[:], in0=pz.t[:], in1=v_.t[:], op=ALU.mult)
        ST, Uw, STb = st["ST"][p2], st["Uw"], st["STb"][p2]
        S.o("act", "activation", [v_], [vbf], out=vbf.t[:], in_=v_.t[:], func=AF.Copy)
        PY = P[6]
        idb = mx.identb
        for cp in range(SEG // 256):
            chains = [(cc, hd) for cc in range(2) for hd in range(2)]
            idbb = mx.identb
            for cc in range(2):
                c = cp * 2 + cc
                cs = slice(c * 128, (c + 1) * 128)
                for k3, src in enumerate((be, ka, vbf)):
                    S.o("pe", "transpose", [src, idbb], [mx.PB], out=mx.PB.t[:, (cc * 3 + k3) * 128:(cc * 3 + k3 + 1) * 128],
                        in_=src.t[:, cs], identity=idbb.t[:])
            for cc in range(2):
                for k3, dst in enumerate((st["btm"][cc], st["ktm"][cc], st["vtm"][cc])):
                    S.o("act", "activation", [mx.PB], [dst], out=dst.t[:],
                        in_=mx.PB.t[:, (cc * 3 + k3) * 128:(cc * 3 + k3 + 1) * 128], func=AF.Copy)
                for hd in range(2):
                    S.o("act", "activation", [mx.PB], [st["Vw"][cc][hd]], out=st["Vw"][cc][hd].t[:, hd * 64:(hd + 1) * 64],
                        in_=mx.PB.t[:, (cc * 3 + 2) * 128 + hd * 64:(cc * 3 + 2) * 128 + (hd + 1) * 64], func=AF.Copy)
            MM, QQ = st["MM"], st["QQ"]
            for ci, (cc, hd) in enumerate(chains):
                c = cp * 2 + cc
                cs = slice(c * 128, (c + 1) * 128)
                hp = slice(hd * 64, (hd + 1) * 64)
                pz = P[ci]
                S.mm(pz.t[:, 0:128], al.t[hp, cs], be.t[hp, cs], True, True, [al, be], [pz])
                S.mm(pz.t[:, 128:256], be.t[hp, cs], al.t[hp, cs], True, True, [al, be], [pz])
                S.o("dve", "tensor_tensor", [pz, st["mask_lu"]], [MM[ci][0]], out=MM[ci][0].t[:],
                    in0=pz.t[:, 0:256].rearrange("p (a b) -> p a b", a=2), in1=st["mask_lu"].t[:], op=ALU.mult)
                S.o("pool", "tensor_tensor", [MM[ci][0], st["id2"]], [QQ[ci][0]], out=QQ[ci][0].t[:], in0=MM[ci][0].t[:],
                    in1=st["id2"].t[:], op=ALU.add)
            for ci, (cc, hd) in enumerate(chains):
                c = cp * 2 + cc
                cs = slice(c * 128, (c + 1) * 128)
                hp = slice(hd * 64, (hd + 1) * 64)
                pz = P[ci]
                S.mm(pz.t[:, 0:128], be.t[hp, cs], rt.t[hp, cs], True, True, [be, rt], [pz])
                S.mm(pz.t[:, 128:256], ka.t[hp, cs], rt.t[hp, cs], True, True, [ka, rt], [pz])
                S.o("dve", "tensor_tensor", [pz, st["mask_uu"]], [st["BB"][ci]], out=st["BB"][ci].t[:],
                    in0=pz.t[:, 0:256].rearrange("p (a b) -> p a b", a=2), in1=st["mask_uu"].t[:], op=ALU.mult)
            for ci, (cc, hd) in enumerate(chains):
                c = cp * 2 + cc
                cs = slice(c * 128, (c + 1) * 128)
                hp = slice(hd * 64, (hd + 1) * 64)
                pz = P[ci]
                S.mm(pz.t[:, 0:128], ka.t[hp, cs], al.t[hp, cs], True, True, [ka, al], [pz])
                S.o("dve", "tensor_tensor", [pz, ustrict], [st["AkT"][ci]], out=st["AkT"][ci].t[:], in0=pz.t[:, 0:128],
                    in1=ustrict.t[:], op=ALU.mult)
            cur = 0
            for lv in range(1, 7):
                nxt = 1 - cur
                for ci in range(4):
                    pz = P[ci]
                    Mc, MTc = MM[ci][cur].t[:, 0, :], MM[ci][cur].t[:, 1, :]
                    S.mm(pz.t[:, 0:128], MTc, Mc, True, True, [MM[ci][cur]], [pz])
                    S.mm(pz.t[:, 128:256], Mc, MTc, True, True, [MM[ci][cur]], [pz])
                    S.o("act", "activation", [pz], [MM[ci][nxt]], out=MM[ci][nxt].t[:],
                        in_=pz.t[:, 0:256].rearrange("p (a b) -> p a b", a=2), func=AF.Copy)
                for ci in range(4):
                    pz = P[ci]
                    Qc, QTc = QQ[ci][cur].t[:, 0, :], QQ[ci][cur].t[:, 1, :]
                    S.mm(pz.t[:, 0:128], QTc, MM[ci][nxt].t[:, 0, :], True, True, [QQ[ci][cur], MM[ci][nxt]], [pz])
                    S.mm(pz.t[:, 128:256], Qc, MM[ci][nxt].t[:, 1, :], True, True, [QQ[ci][cur], MM[ci][nxt]], [pz])
                    S.o("dve", "tensor_tensor", [pz, QQ[ci][cur]], [QQ[ci][nxt]], out=QQ[ci][nxt].t[:],
                        in0=pz.t[:, 0:256].rearrange("p (a b) -> p a b", a=2), in1=QQ[ci][cur].t[:], op=ALU.add)
                cur = nxt
            QTfin = [QQ[ci][cur] for ci in range(4)]
            for cc in range(2):
                c = cp * 2 + cc
                cs = slice(c * 128, (c + 1) * 128)
                Vw = st["Vw"][cc]
                AkT = st["AkT"][cc * 2:cc * 2 + 2]
                BBc = st["BB"][cc * 2:cc * 2 + 2]
                QTf = QTfin[cc * 2:cc * 2 + 2]
                px = nextP()
                S.mm(px.t[:, 0:128], al.t[:, cs], STb.t[:], True, False, [al, STb], [px])
                S.mm(px.t[:, 0:128], AkT[0].t[:], Vw[0].t[:], False, False, [AkT[0], Vw[0]], [px])
                S.mm(px.t[:, 0:128], AkT[1].t[:], Vw[1].t[:], False, True, [AkT[1], Vw[1]], [px])
                S.o("act", "activation", [px], [st["Xs"]], out=st["Xs"].t[:], in_=px.t[:, 0:128], func=AF.Copy)
                pu = nextP()
                for hd in range(2):
                    S.mm(pu.t[:, hd * 64:(hd + 1) * 64], QTf[hd].t[:, 1, :], st["Xs"].t[:, hd * 64:(hd + 1) * 64], True, True,
                         [QTf[hd], st["Xs"]], [pu])
                S.o("act", "activation", [pu], [st["Ua"]], out=st["Ua"].t[:], in_=pu.t[:, 0:128], func=AF.Copy)
                for hd in range(2):
                    S.o("act", "activation", [pu], [Uw[hd]], out=Uw[hd].t[:, hd * 64:(hd + 1) * 64],
                        in_=pu.t[:, hd * 64:(hd + 1) * 64], func=AF.Copy)
                S.mm(PY.t[:, cs], STb.t[:], rt.t[:, cs], True, False, [STb, rt], [PY])
                for hd in range(2):
                    S.mm(PY.t[:, cs], Uw[hd].t[:], BBc[hd].t[:, 0, :], False, False, [Uw[hd], BBc[hd]], [PY])
                    S.mm(PY.t[:, cs], Vw[hd].t[:], BBc[hd].t[:, 1, :], False, hd == 1, [Vw[hd], BBc[hd]], [PY])
                psu = nextP()
                S.mm(psu.t[:, 0:128], st["btm"][cc].t[:], st["Ua"].t[:], True, False, [st["btm"][cc], st["Ua"]], [psu])
                S.mm(psu.t[:, 0:128], st["ktm"][cc].t[:], st["vtm"][cc].t[:], False, True, [st["ktm"][cc], st["vtm"][cc]],
                     [psu])
                for hd in range(2):
                    hp = slice(hd * 64, (hd + 1) * 64)
                    epc = tmp2.t[hp, c * 128 + 127:c * 128 + 128]
                    S.o("dve", "tensor_scalar", [ST, tmp2], [ST], out=ST.t[hp, hp], in0=ST.t[hp, hp], scalar1=epc,
                        scalar2=None, op0=ALU.mult)
                    S.o("dve", "scalar_tensor_tensor", [psu, tmp2, ST], [ST], out=ST.t[hp, hp], in0=psu.t[hp, hp],
                        scalar=epc, in1=ST.t[hp, hp], op0=ALU.mult, op1=ALU.add)
                    S.o("act", "activation", [ST], [STb], out=STb.t[hp, hp], in_=ST.t[hp, hp], func=AF.Copy)
        y = F[0]
        S.o("act", "activation", [PY], [y], out=y.t[:], in_=PY.t[:], func=AF.Copy)
        pz = nextP()
        S.mm(pz.t[:], bonesf.t[:], y.t[:], True, True, [bonesf, y], [pz])
        S.o("dve", "scalar_tensor_tensor", [pz, y], [y], out=y.t[:], in0=pz.t[:], scalar=-1.0 / 64.0, in1=y.t[:],
            op0=ALU.mult, op1=ALU.add)
        S.o("act", "activation", [y], [tmp], out=tmp.t[:], in_=y.t[:], func=AF.Square)
        pz = nextP()
        S.mm(pz.t[:], bonesf.t[:], tmp.t[:], True, True, [bonesf, tmp], [pz])
        S.o("act", "activation", [pz, st["gneps"]], [tmp], out=tmp.t[:], in_=pz.t[:], func=AF.Sqrt, scale=1.0 / 64.0,
            bias=st["gneps"].t[:, 0:1])
        S.o("dve", "reciprocal", [tmp], [tmp], out=tmp.t[:], in_=tmp.t[:])
        S.o("dve", "scalar_tensor_tensor", [y, mx.vec, tmp], [y], out=y.t[:], in0=y.t[:], scalar=mx.v(vb + 18 + p2),
            in1=tmp.t[:], op0=ALU.mult, op1=ALU.mult)
        S.o("dve", "tensor_tensor", [y, bonus], [y], out=y.t[:], in0=y.t[:], in1=bonus.t[:], op=ALU.add)
        ob = B[6 + p2]
        S.o("dve", "tensor_tensor", [y, g_], [ob], out=ob.t[:], in0=y.t[:], in1=g_.t[:], op=ALU.mult)
        S.dma(mx.hid_dst(768 + p2 * 128, sg), ob.t[:], reads=[ob])


def build_fused(T=4096, TP=1024):
    NT = T // 2
    pg = Prog(num_devices=8)
    nc, S = pg.nc, pg.S
    xT = pg.din("xT", [1024, NT])
    memT = pg.din("memT", [1024, 256])
    yT = pg.dout("yT", [1024, NT])
    xbuf = nc.dram_tensor("xbuf", [1024, NT], F32, kind="Internal").ap()
    hown = nc.dram_tensor("hown", [1024, NT], BF16, kind="Internal").ap()
    hsh = [nc.dram_tensor(f"hsh{l}", [2 * 1024, NT], BF16, kind="Internal", addr_space="Shared").ap() for l in range(2)]
    hidsh = [nc.dram_tensor(f"hidsh{l}", [2 * 2 * 1024, NT], BF16, kind="Internal", addr_space="Shared").ap()
             for l in range(2)]
    hid_loc = nc.dram_tensor("hid_loc", [2 * 1024, NT], BF16, kind="Internal").ap()
    hid_mine = nc.dram_tensor("hid_mine", [2 * 1024, NT], BF16, kind="Internal").ap()
    jv = nc.sync.partition_id() % 2

    def core_sync():
        S.barrier()
        nc.all_core_barrier()

    def publish_h(l):
        S.dma(hsh[l][bass.ts(jv, 1024), :], hown)
        core_sync()

    def mixers(l):
        src = hsh[l]
        hsrc = lambda sg: xv(src[(sg // 4) * 1024:(sg // 4 + 1) * 1024, :])[:, :, (sg % 4) * SEG:(sg % 4 + 1) * SEG]
        hid_dst = lambda row0, sg: hid_loc[(sg // 4) * 1024 + row0:(sg // 4) * 1024 + row0 + 128,
                                           (sg % 4) * SEG:(sg % 4 + 1) * SEG]
        pg.pre = f"M{l}_"
        for do in (("ml", "gla"), ("s5", "rw")):
            with pg.phase():
                mix_phase(pg, T, do, hsrc, hid_dst)
        for th in range(2):
            S.dma(hidsh[l][th * 2048:(th + 1) * 2048, :][bass.ts(jv, 1024), :], hid_loc[th * 1024:(th + 1) * 1024, :])
        core_sync()
        S.dma(hid_mine, hidsh[l][bass.ts(jv, 2048), :])
        S.barrier()

    def tok(pre, l_post, do_pre, do_final, first):
        io = {"memT": memT}
        io["x_src"] = (lambda g0, g1: xv(xT)[:, :, g0:g1]) if first else (lambda g0, g1: xv(xbuf)[:, :, g0:g1])
        if l_post is not None:
            io["h_src"] = lambda g0, g1: xv(hown)[:, :, g0:g1]

            def hid_load(S_, at, abuf, sl, g0, g1):
                for r in range(2):
                    for br in range(4):
                        rows = hid_mine[r * 1024 + br * 256: r * 1024 + (br + 1) * 256, :]
                        S_.dma(at[:, br * 4 + r * 2: br * 4 + r * 2 + 2, sl],
                               rows.rearrange("(cc p) n -> p cc n", p=128)[:, :, g0:g1], writes=[abuf])
            io["hid_load"] = hid_load
        if do_pre:
            io["x_dst"] = [lambda g0, g1: xv(xbuf)[:, :, g0:g1]]
            io["h_dst"] = [lambda g0, g1: xv(hown)[:, :, g0:g1]]
        if do_final:
            io["y_dst"] = lambda g0, g1: xv(yT)[:, :, g0:g1]
        pg.pre = pre
        with pg.phase():
            tok_phase(pg, l_post is not None, do_pre, do_final, NT, TP, io)

    tok("A_", None, True, False, True)
    publish_h(0)
    mixers(0)
    tok("B_", 0, True, False, False)
    publish_h(1)
    mixers(1)
    tok("C_", 1, False, True, False)
    S.finish()
    return pg


_PROGS = {}


def _prog(key, fn):
    if key not in _PROGS:
        _PROGS[key] = fn()
    return _PROGS[key]


def kernel(**inputs):
    p = {k: np.asarray(v) for k, v in inputs.items()}
    x, mem = p["x"], p["mem"]
    NB, T, D = x.shape
    NT = T // 2
    pg = _prog("fused", lambda: build_fused(T))
    names = set(pg.dram.keys())
    shared = {}
    for pre, args in (("A_", (None, 0, False)), ("B_", (0, 1, False)), ("C_", (1, None, True))):
        for k, v in tok_weights(p, *args).items():
            shared[k if k in Prog.GLOBAL else pre + k] = v
    maps = []
    dummy_h = None
    for c in range(8):
        b, j = c // 2, c % 2
        m = dict(shared)
        m["xT"] = np.ascontiguousarray(x[b, j * NT:(j + 1) * NT].T)
        m["memT"] = np.ascontiguousarray(mem[b].T)
        for l in range(2):
            for k, v in mix_inputs(p, l, j, None).items():
                if k == "hT":
                    continue
                m[k if k in Prog.GLOBAL else f"M{l}_" + k] = v
        maps.append({k: v for k, v in m.items() if k in names})
    res = run_bass_kernel_spmd(pg.nc, maps, core_ids=list(range(8)))
    out = np.empty((NB, T, D), np.float32)
    for c in range(8):
        out[c // 2, (c % 2) * NT:(c % 2 + 1) * NT] = res.results[c]["yT"].T
    return out
```
